# Optimizing a Trainium2 kernel written in Bass

```python
import jax, jax.numpy as jnp
from jax import lax
import numpy as np

D_MODEL = 1024
BATCH = 8
SEQ = 2048
DEPTH = 4

HEAD_DIM = 64
D_MIX = D_MODEL
A_HEADS = 4
B_HEADS = 4
C_GROUPS = 4
D_GROUPS = 4
A_W = A_HEADS * HEAD_DIM
B_W = B_HEADS * HEAD_DIM
C_W = C_GROUPS * HEAD_DIM
D_W = D_GROUPS * HEAD_DIM
N_NSA_KV = 6
CMP_LEN = 32
CMP_STRIDE = 16
CMP_HIDDEN = 256
SEL_BLOCK = 64
SEL_TOPK = 16
SEL_Q_CHUNK = 64
WIN = 512
FORCE = 1e4
DILATED_PAIRS = ((128, 1), (512, 4), (2048, 16))
BAND_BLOCK = 128
POOL_SIZES = (2, 4, 8, 16)
SG_CHUNK = 128
RMS_EPS = 1e-6
LN_EPS = 1e-5
NEG = -1e30
SPLITS = (A_W, N_NSA_KV * HEAD_DIM, 3 * A_HEADS, A_W, 3 * B_W, B_W, C_W, C_W, 2 * D_W, D_W)
D_IN = A_W + N_NSA_KV * HEAD_DIM + 3 * A_HEADS + A_W + 3 * B_W + B_W + C_W + C_W + 2 * D_W + D_W

kernel_name = "hybrid_nsa_dilated_pool_sgu"


def rmsnorm(x, g):
    xf = x.astype(jnp.float32)
    y = xf * lax.rsqrt(jnp.mean(xf * xf, axis=-1, keepdims=True) + RMS_EPS)
    return (y * g.astype(jnp.float32)).astype(x.dtype)


def banded_attention(q, k, v, max_dist):
    Bn, H, L, Dh = q.shape
    Hk = k.shape[1]
    rep = H // Hk
    blk = BAND_BLOCK
    n_prev = -(-max_dist // blk)
    nb = -(-L // blk)
    Lp = nb * blk
    pad = Lp - L
    q = jnp.pad(q, ((0, 0), (0, 0), (0, pad), (0, 0)))
    k = jnp.pad(k, ((0, 0), (0, 0), (n_prev * blk, pad), (0, 0)))
    v = jnp.pad(v, ((0, 0), (0, 0), (n_prev * blk, pad), (0, 0)))
    qb = q.reshape(Bn, Hk, rep, nb, blk, Dh)
    kb = k.reshape(Bn, Hk, nb + n_prev, blk, Dh)
    vb = v.reshape(Bn, Hk, nb + n_prev, blk, Dh)
    kw = jnp.concatenate([kb[:, :, r:r + nb] for r in range(n_prev + 1)], axis=3)
    vw = jnp.concatenate([vb[:, :, r:r + nb] for r in range(n_prev + 1)], axis=3)
    s = jnp.einsum('bgrnqd,bgnkd->bgrnqk', qb, kw).astype(jnp.float32) * (Dh ** -0.5)
    qpos = jnp.arange(nb)[:, None, None] * blk + jnp.arange(blk)[None, :, None]
    kpos = jnp.arange(nb)[:, None, None] * blk + jnp.arange((n_prev + 1) * blk)[None, None, :] - n_prev * blk
    dist = qpos - kpos
    mask = (dist >= 0) & (dist <= max_dist) & (kpos >= 0)
    s = jnp.where(mask, s, NEG)
    lse = jax.nn.logsumexp(s, axis=-1)
    p = jnp.exp(s - lse[..., None])
    o = jnp.einsum('bgrnqk,bgnkd->bgrnqd', p.astype(vw.dtype), vw)
    o = o.reshape(Bn, H, Lp, Dh)[:, :, :L]
    lse = lse.reshape(Bn, H, Lp)[:, :, :L]
    return o, lse


def nsa_mixer(q, kv, gates, pe_cmp, w_cmp1, w_cmp2):
    Bn, S, H, Dh = q.shape
    scale = Dh ** -0.5
    t = jnp.arange(S)
    n_cmp = (S - CMP_LEN) // CMP_STRIDE + 1
    cidx = jnp.arange(n_cmp)[:, None] * CMP_STRIDE + jnp.arange(CMP_LEN)[None, :]
    kvc = kv[:, :, 0:2][:, cidx] + pe_cmp.transpose(1, 0, 2)
    flat = kvc.transpose(0, 1, 3, 2, 4).reshape(Bn, n_cmp, 2, CMP_LEN * Dh)
    hid = jax.nn.gelu(jnp.einsum('bnjf,jfh->bnjh', flat, w_cmp1))
    comp = jnp.einsum('bnjh,jhd->bnjd', hid, w_cmp2)
    k_c, v_c = comp[:, :, 0], comp[:, :, 1]
    s_c = jnp.einsum('bshd,bnd->bhsn', q, k_c).astype(jnp.float32) * scale
    valid_c = cidx[:, -1][None, :] <= t[:, None]
    p_c = jax.nn.softmax(jnp.where(valid_c, s_c, NEG), axis=-1) * valid_c
    o_cmp = jnp.einsum('bhsn,bnd->bshd', p_c.astype(v_c.dtype), v_c)
    n_slc = S // SEL_BLOCK
    cmp_start = jnp.arange(n_cmp) * CMP_STRIDE
    sel_start = jnp.arange(n_slc) * SEL_BLOCK
    overlap = ((cmp_start[:, None] < sel_start[None, :] + SEL_BLOCK)
               & (cmp_start[:, None] + CMP_LEN > sel_start[None, :])).astype(jnp.float32)
    imp = jnp.einsum('bhsn,nj->bsj', p_c, overlap)
    cur = t // SEL_BLOCK
    j = jnp.arange(n_slc)
    forced = (j[None, :] == 0) | (j[None, :] == cur[:, None]) | (j[None, :] == cur[:, None] - 1)
    valid_s = j[None, :] <= cur[:, None]
    imp = jnp.where(forced, FORCE, jnp.where(valid_s, imp, -FORCE))
    k_top = min(SEL_TOPK, n_slc)
    _, sel_idx = lax.top_k(imp, k_top)
    kb = kv[:, :, 2].reshape(Bn, n_slc, SEL_BLOCK, Dh)
    vb = kv[:, :, 3].reshape(Bn, n_slc, SEL_BLOCK, Dh)
    nq = S // SEL_Q_CHUNK
    qc = q.reshape(Bn, nq, SEL_Q_CHUNK, H, Dh).transpose(1, 0, 2, 3, 4)
    ic = sel_idx.reshape(Bn, nq, SEL_Q_CHUNK, k_top).transpose(1, 0, 2, 3)
    tc = t.reshape(nq, SEL_Q_CHUNK)

    def sel_chunk(args):
        qq, ii, tt = args
        kg = jax.vmap(lambda a, b: a[b])(kb, ii)
        vg = jax.vmap(lambda a, b: a[b])(vb, ii)
        kpos = ii[..., None] * SEL_BLOCK + jnp.arange(SEL_BLOCK)
        ok = kpos <= tt[None, :, None, None]
        s = jnp.einsum('bchd,bckld->bhckl', qq, kg).astype(jnp.float32) * scale
        s = jnp.where(ok[:, None], s, NEG).reshape(Bn, H, SEL_Q_CHUNK, k_top * SEL_BLOCK)
        p = jax.nn.softmax(s, axis=-1)
        return jnp.einsum('bhcn,bcnd->bchd', p.astype(vg.dtype),
                          vg.reshape(Bn, SEL_Q_CHUNK, k_top * SEL_BLOCK, Dh))

    o_slc = lax.map(sel_chunk, (qc, ic, tc))
    o_slc = o_slc.transpose(1, 0, 2, 3, 4).reshape(Bn, S, H, Dh)
    o_win, _ = banded_attention(q.transpose(0, 2, 1, 3), kv[:, :, 4][:, None], kv[:, :, 5][:, None], WIN - 1)
    o_win = o_win.transpose(0, 2, 1, 3)
    g = jax.nn.sigmoid(gates.astype(jnp.float32))[..., None]
    o = (g[:, :, 0] * o_cmp.astype(jnp.float32) + g[:, :, 1] * o_slc.astype(jnp.float32)
         + g[:, :, 2] * o_win.astype(jnp.float32))
    return o.reshape(Bn, S, H * Dh)


def dilated_mixer(q, k, v):
    Bn, H, S, Dh = q.shape
    outs, lses = [], []
    for window, dil in DILATED_PAIRS:
        L = S // dil

        def by_stride(a):
            return a.reshape(Bn, H, L, dil, Dh).transpose(0, 1, 3, 2, 4).reshape(Bn, H * dil, L, Dh)

        o, lse = banded_attention(by_stride(q), by_stride(k), by_stride(v), window // dil)
        outs.append(o.reshape(Bn, H, dil, L, Dh).transpose(0, 1, 3, 2, 4).reshape(Bn, H, S, Dh))
        lses.append(lse.reshape(Bn, H, dil, L).transpose(0, 1, 3, 2).reshape(Bn, H, S))
    w = jax.nn.softmax(jnp.stack(lses, axis=0), axis=0)
    o = jnp.sum(w[..., None] * jnp.stack(outs, axis=0).astype(jnp.float32), axis=0)
    return o.transpose(0, 2, 1, 3).reshape(Bn, S, H * Dh)


def pool_mixer(c, w_pool, pool_scale):
    Bn, S, _ = c.shape
    cf = c.astype(jnp.float32).reshape(Bn, S, C_GROUPS, HEAD_DIM)
    cs = jnp.pad(jnp.cumsum(cf, axis=1), ((0, 0), (1, 0), (0, 0), (0, 0)))
    t = jnp.arange(S)
    outs = []
    for g, w in enumerate(POOL_SIZES):
        lo = jnp.maximum(t + 1 - w, 0)
        win_sum = cs[:, 1:, g] - cs[:, lo, g]
        cnt = (t + 1 - lo).astype(jnp.float32)
        outs.append(win_sum / cnt[None, :, None] - cf[:, :, g])
    pooled = jnp.stack(outs, axis=2)
    mixed = jnp.einsum('bsgc,gcd->bsgd', pooled, w_pool.astype(jnp.float32))
    return mixed.reshape(Bn, S, C_W) * pool_scale.astype(jnp.float32)


def spatial_gating(uv, ln_g, ln_b, w_sp, b_sp):
    Bn, S, _ = uv.shape
    u, v = jnp.split(uv, 2, axis=-1)
    vf = v.astype(jnp.float32)
    mu = jnp.mean(vf, axis=-1, keepdims=True)
    var = jnp.mean(jnp.square(vf - mu), axis=-1, keepdims=True)
    vn = (vf - mu) * lax.rsqrt(var + LN_EPS) * ln_g.astype(jnp.float32) + ln_b.astype(jnp.float32)
    nc = S // SG_CHUNK
    vn = vn.reshape(Bn, nc, SG_CHUNK, D_GROUPS, HEAD_DIM)
    w = w_sp.astype(jnp.float32) * jnp.tril(jnp.ones((SG_CHUNK, SG_CHUNK), jnp.float32))
    z = jnp.einsum('gij,bnjgc->bnigc', w, vn) + b_sp.astype(jnp.float32).T[None, None, :, :, None]
    return u.astype(jnp.float32) * z.reshape(Bn, S, D_W)


def hybrid_layer(x, g_pre, w_in, pe_cmp, w_cmp1, w_cmp2, w_pool, pool_scale,
                 sg_ln_g, sg_ln_b, w_sp, b_sp, w_out, g_post):
    Bn, S, _ = x.shape
    h = rmsnorm(x, g_pre)
    proj = h @ w_in
    split_points = np.cumsum(np.array(SPLITS))[:-1].tolist()
    a_q, a_kv, a_g, a_z, b_qkv, b_z, c_in, c_z, d_uv, d_z = jnp.split(proj, split_points, axis=-1)
    y_a = nsa_mixer(a_q.reshape(Bn, S, A_HEADS, HEAD_DIM), a_kv.reshape(Bn, S, N_NSA_KV, HEAD_DIM),
                    a_g.reshape(Bn, S, 3, A_HEADS), pe_cmp, w_cmp1, w_cmp2)
    bqkv = b_qkv.reshape(Bn, S, 3, B_HEADS, HEAD_DIM).transpose(2, 0, 3, 1, 4)
    y_b = dilated_mixer(bqkv[0], bqkv[1], bqkv[2])
    y_c = pool_mixer(c_in, w_pool, pool_scale)
    y_d = spatial_gating(d_uv, sg_ln_g, sg_ln_b, w_sp, b_sp)
    y = jnp.concatenate([
        (y_a * jax.nn.silu(a_z.astype(jnp.float32))).astype(x.dtype),
        (y_b * jax.nn.silu(b_z.astype(jnp.float32))).astype(x.dtype),
        (y_c * jax.nn.silu(c_z.astype(jnp.float32))).astype(x.dtype),
        (y_d * jax.nn.silu(d_z.astype(jnp.float32))).astype(x.dtype),
    ], axis=-1)
    out = y @ w_out
    return x + rmsnorm(out, g_post)


def setup_inputs(seed: int = 0) -> dict:
    key = jax.random.key(seed)
    ks = jax.random.split(key, 16)
    f32 = jnp.float32
    n = lambda k, shape: jax.random.normal(k, shape, f32)
    return {
        "x": n(ks[0], (BATCH, SEQ, D_MODEL)),
        "g_pre": 1.0 + 0.1 * n(ks[1], (DEPTH, D_MODEL)),
        "w_in": n(ks[2], (DEPTH, D_MODEL, D_IN)) * D_MODEL ** -0.5,
        "pe_cmp": 0.2 * n(ks[3], (DEPTH, 2, CMP_LEN, HEAD_DIM)),
        "w_cmp1": n(ks[4], (DEPTH, 2, CMP_LEN * HEAD_DIM, CMP_HIDDEN)) * (CMP_LEN * HEAD_DIM) ** -0.5,
        "w_cmp2": n(ks[5], (DEPTH, 2, CMP_HIDDEN, HEAD_DIM)) * CMP_HIDDEN ** -0.5,
        "w_pool": n(ks[6], (DEPTH, C_GROUPS, HEAD_DIM, HEAD_DIM)) * HEAD_DIM ** -0.5,
        "pool_scale": 1.0 + 0.1 * n(ks[7], (DEPTH, C_W)),
        "sg_ln_g": 1.0 + 0.1 * n(ks[8], (DEPTH, D_W)),
        "sg_ln_b": 0.02 * n(ks[9], (DEPTH, D_W)),
        "w_sp": n(ks[10], (DEPTH, D_GROUPS, SG_CHUNK, SG_CHUNK)) * SG_CHUNK ** -0.5,
        "b_sp": 1.0 + 0.1 * n(ks[11], (DEPTH, D_GROUPS, SG_CHUNK)),
        "w_out": n(ks[12], (DEPTH, D_MIX, D_MODEL)) * D_MIX ** -0.5,
        "g_post": 1.0 + 0.1 * n(ks[13], (DEPTH, D_MODEL)),
    }


def reference(x, g_pre, w_in, pe_cmp, w_cmp1, w_cmp2, w_pool, pool_scale,
              sg_ln_g, sg_ln_b, w_sp, b_sp, w_out, g_post):
    for l in range(DEPTH):
        x = hybrid_layer(x, g_pre[l], w_in[l], pe_cmp[l], w_cmp1[l], w_cmp2[l], w_pool[l], pool_scale[l],
                         sg_ln_g[l], sg_ln_b[l], w_sp[l], b_sp[l], w_out[l], g_post[l])
    return x
```

```python
import numpy as np
import ml_dtypes
import concourse.bass as bass
import concourse.mybir as mybir
from concourse.bass_utils import run_bass_kernel_spmd

F32 = mybir.dt.float32
BF16 = mybir.dt.bfloat16
AF = mybir.ActivationFunctionType
ALU = mybir.AluOpType

S = 2048
D = 1024
NT = 16
KC = 8
DIN = 3212
DEPTH = 4
NCORES = 8
RMS_EPS = 1e-6
LN_EPS = 1e-5
POOLS = (2, 4, 8, 16)
C_AQ, C_KC, C_VC, C_KS, C_VS, C_KW, C_VW, C_AG, C_AZ = 0, 256, 320, 384, 448, 512, 576, 640, 652
C_BQ, C_BK, C_BV, C_BZ, C_CIN, C_CZ, C_DU, C_DV, C_DZ = 908, 1164, 1420, 1676, 1932, 2188, 2444, 2700, 2956
BIGNEG = -240000.0


class Prog:
    def __init__(self, nc):
        self.nc = nc
        self.eng = dict(pe=nc.tensor, act=nc.scalar, dve=nc.vector, pool=nc.gpsimd, sp=nc.sync)
        self.sem = {e: nc.alloc_semaphore(name=f"sem_{e}") for e in self.eng}
        self.cnt = {e: 0 for e in self.eng}
        self.seen = {e: {} for e in self.eng}
        self.lw = {}
        self.rd = {}
        self.dsem = {}

    def _handle(self, sk):
        return self.sem[sk] if sk in self.sem else self.dsem[sk][0]

    def _deps(self, e, reads, writes):
        need = {}

        def add(t):
            if need.get(t[0], 0) < t[1]:
                need[t[0]] = t[1]
        for r in reads:
            t = self.lw.get(r)
            if t:
                add(t)
        for w in writes:
            t = self.lw.get(w)
            if t and t[0] != e:
                add(t)
            for sk, v in self.rd.get(w, {}).items():
                if sk != e:
                    add((sk, v))
        for sk, val in need.items():
            if self.seen[e].get(sk, 0) >= val:
                continue
            self.seen[e][sk] = val
            self.eng[e].wait_ge(self._handle(sk), val)

    def _commit(self, tok, reads, writes):
        for w in writes:
            self.lw[w] = tok
            self.rd[w] = {}
        for r in reads:
            d = self.rd.setdefault(r, {})
            if d.get(tok[0], 0) < tok[1]:
                d[tok[0]] = tok[1]

    def op(self, e, fn, reads=(), writes=()):
        self._deps(e, reads, writes)
        inst = fn(self.eng[e])
        self.cnt[e] += 1
        inst.then_inc(self.sem[e], 1)
        tok = (e, self.cnt[e])
        self._commit(tok, reads, writes)
        return tok

    def dma(self, q, pairs, semname, reads=(), writes=(), **kw):
        if semname not in self.dsem:
            self.dsem[semname] = [self.nc.alloc_semaphore(name=f"d_{semname}"), 0]
        self._deps(q, reads, writes)
        h = self.dsem[semname]
        for (o, i) in pairs:
            inst = self.eng[q].dma_start(out=o, in_=i, **kw)
            h[1] += 16
            inst.then_inc(h[0], 16)
        tok = (semname, h[1])
        self._commit(tok, reads, writes)
        return tok

    def barrier(self):
        ces = ['pe', 'act', 'dve']
        for e in ces + ['sp']:
            for e2 in ces + ['pool']:
                if e2 == e:
                    continue
                v = self.cnt[e2]
                if v > self.seen[e].get(e2, 0):
                    self.seen[e][e2] = v
                    self.eng[e].wait_ge(self.sem[e2], v)


def host_consts():
    bf = ml_dtypes.bfloat16
    c = {}
    c["identb"] = np.eye(128, dtype=np.float32).astype(bf)
    c["identf"] = np.eye(128, dtype=np.float32)
    k = np.arange(128)[:, None, None]
    e = np.arange(16)[None, :, None]
    q = np.arange(128)[None, None, :]
    d = (15 - e) * 128 + q - k
    mult = ((d <= 128).astype(np.float32) + ((d % 4 == 0) & (d <= 512)) + (d % 16 == 0)) * (d >= 0)
    c["mtab"] = mult.astype(bf)
    n = np.arange(128)[:, None, None]
    qt = np.arange(16)[None, :, None]
    mc = ((16 * n + 31 <= 128 * qt + q) & (n < 127)).astype(np.float32)
    c["mc"] = ((mc - 1.0) * 240000.0).astype(bf)
    kk = np.arange(128)[:, None]
    qq = np.arange(128)[None, :]
    tri = np.zeros((128, 2, 128), np.float32)
    tri[:, 0, :] = (kk <= qq)
    tri[:, 1, :] = (kk > qq)
    c["tri"] = ((tri - 1.0) * 240000.0).astype(bf)
    es = np.zeros((128, 16, 128), np.float32)
    for kt in range(16):
        for kk_ in range(128):
            j = 2 * kt + (1 if kk_ >= 64 else 0)
            es[j, kt, kk_] = 1.0
            es[64 + j, kt, kk_] = 1.0
    ef = np.zeros((64, 2048), np.float32)
    for j in range(32):
        ef[j, j * 64:(j + 1) * 64] = 1.0
    c["efull"] = ef.astype(bf)
    p = np.arange(128)[:, None, None]
    t = 128 * qt + p
    cur = t // 64
    j = np.arange(32)[None, None, :]
    forced = (j == 0) | (j == cur) | (j == cur - 1)
    valid = j <= cur
    tka = (valid & ~forced).astype(np.float32)
    tkb = np.where(forced, 8192.0, np.where(valid, 0.0, -8192.0)).astype(np.float32)
    c["tka"] = tka.astype(bf)
    c["tkb"] = tkb.astype(bf)
    c["trilT"] = (kk <= qq).astype(np.float32)
    invcnt = np.zeros((128, 2, 16), np.float32)
    invw = np.zeros((128, 2), np.float32)
    for cc in range(2):
        for pp in range(128):
            w = POOLS[2 * cc + pp // 64]
            invw[pp, cc] = 1.0 / w
            for tt in range(16):
                invcnt[pp, cc, tt] = 1.0 / min(w, tt + 1)
    tp_ = np.arange(128)[:, None]
    tt_ = np.arange(128)[None, :]
    pm = np.zeros((128, 4, 4, 128), np.float32)
    for g_ in range(4):
        w_ = POOLS[g_]
        dd_ = tt_ - tp_
        same = ((dd_ >= 0) & (dd_ < w_)).astype(np.float64)
        pm[:, 0, g_, :] = same / w_ - (dd_ == 0)
        ddp = tt_ + 128 - tp_
        pm[:, 1, g_, :] = ((ddp >= 0) & (ddp < w_)).astype(np.float64) / w_
        first = same / np.minimum(w_, tt_ + 1) - (dd_ == 0)
        hi = first.astype(np.float32).astype(bf).astype(np.float64)
        pm[:, 2, g_, :] = hi
        pm[:, 3, g_, :] = first - hi
    c["pm"] = pm.astype(bf)
    ovl = np.zeros((128, 33), np.float32)
    ovl[:127, 0] = 1.0
    for nn in range(127):
        for jj in range(32):
            if (16 * nn < 64 * jj + 64) and (16 * nn + 32 > 64 * jj):
                ovl[nn, 1 + jj] = 1.0
    c["ovl"] = ovl.astype(bf)
    return c


CONST_SPECS = [
    ("identb", [128, 128], BF16), ("identf", [128, 128], F32), ("mtab", [128, 16, 128], BF16),
    ("mc", [128, 16, 128], BF16), ("tri", [128, 2, 128], BF16),
    ("tka", [128, 16, 32], BF16), ("tkb", [128, 16, 32], BF16), ("trilT", [128, 128], F32),
    ("ovl", [128, 33], BF16),
]
W_SPECS = [
    ("g_pre", [D]), ("w_in", [D, DIN]), ("pe_cmp", [2, 32, 64]), ("w_cmp1", [2, 2048, 256]),
    ("w_cmp2", [2, 256, 64]), ("w_pool", [4, 64, 64]), ("pool_scale", [256]), ("sg_ln_g", [256]),
    ("sg_ln_b", [256]), ("w_sp", [4, 128, 128]), ("b_sp", [4, 128]), ("w_out", [D, D]), ("g_post", [D]),
]


def build(n_layers, phases="CDBA", dbg=False):
    nc = bass.Bass("TRN2", target_bir_lowering=False)
    P = Prog(nc)
    x_d = nc.dram_tensor("x", [S, D], F32, kind="ExternalInput").ap()
    out_d = nc.dram_tensor("out", [S, D], F32, kind="ExternalOutput").ap()
    Wd = {}
    for name, shp in W_SPECS:
        Wd[name] = nc.dram_tensor(name, [n_layers] + shp, F32, kind="ExternalInput").ap()
    Cd = {}
    for name, shp, dt in CONST_SPECS:
        Cd[name] = nc.dram_tensor("c_" + name, shp, dt, kind="ExternalInput").ap()
    Cd["efull"] = nc.dram_tensor("c_efull", [64, 2048], BF16, kind="ExternalInput").ap()
    Cd["pm"] = nc.dram_tensor("c_pm", [128, 4, 4, 128], BF16, kind="ExternalInput").ap()
    if dbg:
        dbg_y = nc.dram_tensor("dbg_y", [128, 8, 2048], BF16, kind="ExternalOutput").ap()

    sb = nc.alloc_sbuf_tensor
    x_sb = sb("x_sb", [128, NT, D], F32)
    hT = sb("hT", [128, KC, S], BF16)
    yT = sb("yT", [128, KC, S], BF16)
    wslot = [sb(f"wslot{i}", [128, 2048], BF16) for i in range(3)]
    C = {}
    for name, shp, dt in CONST_SPECS:
        C[name] = sb("k_" + name, shp, dt)
    gpreT = sb("gpreT", [128, n_layers, 8], F32)
    pscale = sb("pscale", [128, n_layers, 2], F32)
    lngb = sb("lngb", [128, 2, 256], F32)
    Bsp = sb("Bsp", [128, 2, 128], F32)
    wpool_bd = sb("wpool_bd", [128, 2, 128], BF16)
    wpool_f = sb("wpool_f", [128, 2, 128], F32)
    WT = sb("WT", [128, 4, 128], BF16)
    W2k = sb("W2k", [128, 2, 128], BF16)
    W2v = sb("W2v", [128, 2, 64], BF16)
    pe2 = sb("pe2", [128, 2, 16], F32)
    kcT = sb("kcT", [128, 128], BF16)
    vcx = sb("vcx", [128, 97], BF16)
    small = sb("small", [128, 104], F32)
    sm2 = sb("sm2", [128, 176], F32)
    sg = sb("sg", [128, 16, 12], F32)
    NPOOL = 24128
    pool = sb("pool", [128, NPOOL], BF16)
    psall = nc.alloc_psum_tensor("psall", [128, 4096], F32)
    PS = [psall[:, i * 512:(i + 1) * 512] for i in range(8)]

    def carve(off_b, shape, dt, p0=0):
        n = int(np.prod(shape[1:]))
        assert off_b % 4 == 0
        if dt == F32:
            assert off_b // 2 + 2 * n <= NPOOL, (off_b, shape)
            ap = pool[p0:p0 + shape[0], off_b // 2: off_b // 2 + 2 * n].bitcast(F32)
        else:
            assert off_b // 2 + n <= NPOOL, (off_b, shape)
            ap = pool[p0:p0 + shape[0], off_b // 2: off_b // 2 + n]
        if len(shape) == 3:
            ap = ap.rearrange("p (a b) -> p a b", a=shape[1])
        elif len(shape) == 4:
            ap = ap.rearrange("p (a b c) -> p a b c", a=shape[1], b=shape[2])
        return ap

    def psb(bank):
        return PS[bank][:, :].bitcast(BF16)

    for i in range(2):
        P.dma('sp', [(x_sb[:, i, :], x_d[i * 128:(i + 1) * 128, :])], f'x{i}', writes=[('x', i)])
    pairs = [(C[name][:], Cd[name]) for name, _, _ in CONST_SPECS]
    P.dma('sp', pairs, 'const', writes=['const'])
    with nc.allow_non_contiguous_dma(reason="tiny param transposes"):
        P.dma('act', [(gpreT[:, l, :], Wd["g_pre"][l].rearrange("(kc p) -> p kc", p=128)) for l in range(n_layers)]
              + [(pscale[:, l, :], Wd["pool_scale"][l].rearrange("(cc p) -> p cc", p=128)) for l in range(n_layers)],
              'params', writes=['params'])

    for i in range(2, NT):
        P.dma('sp', [(x_sb[:, i, :], x_d[i * 128:(i + 1) * 128, :])], f'x{i}', writes=[('x', i)])
    P.op('dve', lambda g: g.memset(kcT[:], 0.0), writes=['kcT'])
    P.op('dve', lambda g: g.memset(vcx[:], 0.0), writes=['vcx'])
    P.op('dve', lambda g: g.memset(W2k[:], 0.0), writes=['W2k'])
    wstate = {'n': 0, 'issued': 0}
    wjobs = []

    def plan_jobs():
        for l_ in range(n_layers):
            if "C" in phases:
                wjobs.append((l_, 'in', [(C_CZ, 256, 0)]))
                wjobs.append((l_, 'in', [(C_CIN, 256, 0)]))
            if "D" in phases:
                wjobs.append((l_, 'in', [(C_DV, 256, 0)]))
                wjobs.append((l_, 'in', [(C_DZ, 256, 0)]))
                wjobs.append((l_, 'in', [(C_DU, 256, 0)]))
            if "B" in phases:
                wjobs.append((l_, 'in', [(C_BV, 256, 0)]))
                wjobs.append((l_, 'in', [(C_BZ, 256, 0)]))
                wjobs.append((l_, 'in', [(C_BQ, 256, 0)]))
                wjobs.append((l_, 'in', [(C_BK, 256, 0)]))
            if "A" in phases or "a" in phases:
                wjobs.append((l_, 'in', [(C_KC, 64, 0), (C_KC, 64, 64), (C_VC, 64, 128), (C_VC, 64, 192)]))
                for c_ in range(2):
                    for hh_ in range(2):
                        wjobs.append((l_, 'w1', (c_, hh_)))
            if "A" in phases:
                wjobs.append((l_, 'in', [(C_VS, 64, 0), (C_VW, 64, 64), (C_AG, 12, 128)]))
                wjobs.append((l_, 'in', [(C_AZ, 256, 0)]))
                wjobs.append((l_, 'in', [(C_AQ, 256, 0)]))
                wjobs.append((l_, 'in', [(C_AQ + 64, 64, 0), (C_AQ, 64, 64), (C_AQ + 192, 64, 128), (C_AQ + 128, 64, 192)]))
                wjobs.append((l_, 'in', [(C_KS, 64, 0), (C_KS, 64, 64), (C_KW, 64, 128), (C_KW, 64, 192)]))
    plan_jobs()

    def wview(j):
        l_, kind, pay = wjobs[j]
        s_ = j % 3
        if kind == 'in':
            W = sum(sg_[1] for sg_ in pay)
            return s_, wslot[s_][:, 0:8 * W].rearrange("p (kc c) -> p kc c", kc=8)
        return s_, wslot[s_][:, :].rearrange("p (jp h) -> p jp h", jp=16)

    def issue_w(j):
        l_, kind, pay = wjobs[j]
        s_, view = wview(j)
        if kind == 'in':
            prs = [(view[:, :, d0:d0 + ncol], Wd["w_in"][l_][:, c0:c0 + ncol].rearrange("(kc p) c -> p kc c", p=128))
                   for (c0, ncol, d0) in pay]
        else:
            c_, hh_ = pay
            prs = [(view, Wd["w_cmp1"][l_, c_][:, hh_ * 128:(hh_ + 1) * 128].rearrange("(jp p) h -> p jp h", p=128))]
        P.dma('pool', prs, f'w{s_}', writes=[('w', s_)])

    def next_w(l_, kind, pay):
        j = wstate['n']
        wstate['n'] += 1
        assert wjobs[j] == (l_, kind, pay), (j, wjobs[j], (l_, kind, pay))
        while wstate['issued'] < min(len(wjobs), j + 3):
            issue_w(wstate['issued'])
            wstate['issued'] += 1
        return wview(j)

    def load_w(l, segs, src="w_in"):
        return next_w(l, 'in', segs)

    pbank = {'n': 0}

    def proj_fm(s, wv, nch, evac):
        for ch in range(nch):
            for tb in range(4):
                bank = pbank['n'] % 2
                pbank['n'] += 1

                def mm(e, ch=ch, tb=tb, bank=bank):
                    for kc in range(8):
                        last = e.matmul(PS[bank][:, 0:512], lhsT=wv[:, kc, ch * 128:(ch + 1) * 128],
                                        rhs=hT[:, kc, tb * 512:(tb + 1) * 512], start=(kc == 0), stop=(kc == 7))
                    return last
                P.op('pe', mm, reads=[('w', s), 'hT'], writes=[('ps', bank)])
                evac(ch, tb, bank)

    def proj_tm(s, wv, N, evac):
        for i in range(NT):
            bank = pbank['n'] % 2
            pbank['n'] += 1

            def mm(e, i=i, bank=bank):
                for kc in range(8):
                    last = e.matmul(PS[bank][:, 0:N], lhsT=hT[:, kc, i * 128:(i + 1) * 128],
                                    rhs=wv[:, kc, 0:N], start=(kc == 0), stop=(kc == 7))
                return last
            P.op('pe', mm, reads=[('w', s), 'hT'], writes=[('ps', bank)])
            evac(i, bank)

    altc = {'n': 0}

    def alt(engs=('dve', 'act')):
        altc['n'] += 1
        return engs[altc['n'] % len(engs)]

    def copy_op(e, out, in_, reads, writes):
        if e == 'act':
            P.op('act', lambda g: g.activation(out=out, in_=in_, func=AF.Copy), reads=reads, writes=writes)
        else:
            P.op(e, lambda g: g.tensor_copy(out=out, in_=in_), reads=reads, writes=writes)

    for l in range(n_layers):
        hb = [carve(0, [128, 1024], BF16), carve(2048, [128, 1024], BF16)]
        junk = carve(4096, [128, 1024], BF16)
        ss = small[:, 0:16]
        rs = small[:, 16:32]
        rs2 = small[:, 32:48]
        def p0_a(i):
            P.op('act', lambda g, i=i: g.activation(out=junk, in_=x_sb[:, i, :], func=AF.Square,
                                                    accum_out=ss[:, i:i + 1]),
                 reads=[('x', i)], writes=['junk', ('ss', i)])
            P.op('dve', lambda g, i=i: g.tensor_scalar(out=rs[:, i:i + 1], in0=ss[:, i:i + 1], scalar1=1.0 / D,
                                                       scalar2=RMS_EPS, op0=ALU.mult, op1=ALU.add),
                 reads=[('ss', i)], writes=[('rs', i)])
            P.op('act', lambda g, i=i: g.activation(out=rs2[:, i:i + 1], in_=rs[:, i:i + 1], func=AF.Sqrt),
                 reads=[('rs', i)], writes=[('rs2', i)])
            P.op('dve', lambda g, i=i: g.reciprocal(out=rs[:, i:i + 1], in_=rs2[:, i:i + 1]),
                 reads=[('rs2', i)], writes=[('rs', i)])
            b = i % 2
            P.op('act', lambda g, i=i, b=b: g.activation(out=hb[b], in_=x_sb[:, i, :], func=AF.Copy, scale=rs[:, i:i + 1]),
                 reads=[('x', i), ('rs', i)], writes=[('hb', b)])
            bank = 2 + (i % 2)

            def tr(e, b=b, bank=bank):
                for kc in range(8):
                    last = e.transpose(psb(bank)[:, kc * 128:(kc + 1) * 128], hb[b][:, kc * 128:(kc + 1) * 128],
                                       C["identb"][:])
                return last
            P.op('pe', tr, reads=[('hb', b), 'const'], writes=[('ps', bank)])

        def p0_b(i):
            bank = 2 + (i % 2)
            P.op('dve', lambda g, i=i, bank=bank, l=l: g.tensor_tensor(
                out=hT[:, :, i * 128:(i + 1) * 128],
                in0=psb(bank)[:, 0:1024].rearrange("p (a b) -> p a b", a=8),
                in1=gpreT[:, l, :].unsqueeze(2).broadcast_to([128, 8, 128]), op=ALU.mult),
                reads=[('ps', bank), 'params'], writes=['hT', 'wout', 'gpb'])
        p0_a(0)
        for i in range(NT):
            if i + 1 < NT:
                p0_a(i + 1)
            p0_b(i)
        P.barrier()

        if "C" in phases:
            szc = carve(0, [128, 2, 2048], BF16)
            c_tm = carve(8192, [128, 16, 256], BF16)
            pooledT = carve(16384, [128, 2, 2048], BF16)
            pmv = carve(24576, [128, 4, 4, 128], BF16)
            P.dma('sp', [(pmv, Cd["pm"])], f'lp{l}_6', writes=['pmv'])
            P.op('dve', lambda g: g.memset(wpool_f[:], 0.0), writes=['wpool_f'])
            P.dma('sp', [(wpool_f[(g % 2) * 64:(g % 2) * 64 + 64, g // 2, (g % 2) * 64:(g % 2) * 64 + 64],
                          Wd["w_pool"][l, g]) for g in range(4)], f'lp{l}_1', writes=['wpool_f'])
            P.op('dve', lambda g: g.tensor_copy(out=wpool_bd[:], in_=wpool_f[:]), reads=['wpool_f'], writes=['wpool_bd'])
            s, wv = load_w(l, [(C_CZ, 256, 0)])
            proj_fm(s, wv, 2, lambda ch, tb, bank: P.op(
                'act', lambda g: g.activation(out=szc[:, ch, tb * 512:(tb + 1) * 512], in_=PS[bank][:, 0:512], func=AF.Silu),
                reads=[('ps', bank)], writes=['szc']))
            s, wv = load_w(l, [(C_CIN, 256, 0)])
            proj_tm(s, wv, 256, lambda i, bank: copy_op(
                alt(), c_tm[:, i, :], PS[bank][:, 0:256], reads=[('ps', bank)], writes=[('c_tm', i)]))
            for i2 in range(8):
                bank = 2 + i2 % 2
                pp = PS[bank][:, 0:512].rearrange("p (cc t q) -> p cc t q", cc=2, t=2)

                def mmp(e, i2=i2, pp=pp):
                    for ti in range(2):
                        i = 2 * i2 + ti
                        for g_ in range(4):
                            o = pp[(g_ % 2) * 64:(g_ % 2) * 64 + 64, g_ // 2, ti, :]
                            lh = c_tm[:, i, g_ * 64:(g_ + 1) * 64]
                            if i == 0:
                                e.matmul(o, lhsT=lh, rhs=pmv[:, 2, g_, :], start=True, stop=False)
                                last = e.matmul(o, lhsT=lh, rhs=pmv[:, 3, g_, :], start=False, stop=True)
                            else:
                                e.matmul(o, lhsT=lh, rhs=pmv[:, 0, g_, :], start=True, stop=False)
                                last = e.matmul(o, lhsT=c_tm[:, i - 1, g_ * 64:(g_ + 1) * 64], rhs=pmv[:, 1, g_, :],
                                                start=False, stop=True)
                    return last
                P.op('pe', mmp, reads=[('c_tm', j_) for j_ in range(max(0, 2 * i2 - 1), 2 * i2 + 2)] + ['pmv'],
                     writes=[('ps', bank)])
                copy_op(alt(), pooledT[:, :, i2 * 256:(i2 + 1) * 256], PS[bank][:, 0:512].rearrange("p (cc q) -> p cc q", cc=2),
                        reads=[('ps', bank)], writes=[('pooledT', i2 // 2)])
            for cc in range(2):
                for tb in range(4):
                    bank = pbank['n'] % 2
                    pbank['n'] += 1
                    P.op('pe', lambda e, cc=cc, tb=tb, bank=bank: e.matmul(
                        PS[bank][:, 0:512], lhsT=wpool_bd[:, cc, :], rhs=pooledT[:, cc, tb * 512:(tb + 1) * 512],
                        start=True, stop=True), reads=['wpool_bd', ('pooledT', tb)], writes=[('ps', bank)])
                    P.op('dve', lambda g, cc=cc, tb=tb, bank=bank, l=l: g.scalar_tensor_tensor(
                        out=yT[:, 4 + cc, tb * 512:(tb + 1) * 512], in0=PS[bank][:, 0:512],
                        scalar=pscale[:, l, cc:cc + 1], in1=szc[:, cc, tb * 512:(tb + 1) * 512],
                        op0=ALU.mult, op1=ALU.mult),
                        reads=[('ps', bank), 'params', 'szc'], writes=[('yT', 4 + cc)])
            P.barrier()

        if "D" in phases:
            vn = carve(0, [128, 16, 256], BF16)
            ug = carve(8192, [128, 2, 2048], BF16)
            wspf = carve(16384, [128, 4, 128], F32)
            tmpv = [carve(18432 + k * 1024, [128, 256], F32) for k in range(4)]
            tmpz = [carve(22528 + k * 1024, [128, 2, 128], F32) for k in range(4)]
            st6 = small[:, 64:70]
            mv = small[:, 70:72]
            rstd = small[:, 72:73]
            rstd2 = small[:, 73:74]
            P.dma('sp', [(lngb[:, 0, :], Wd["sg_ln_g"][l:l + 1, :].partition_broadcast(128)) if False else
                         (lngb[:, 0, :], Wd["sg_ln_g"][l].partition_broadcast(128)),
                         (lngb[:, 1, :], Wd["sg_ln_b"][l].partition_broadcast(128))]
                  + [(Bsp[(g % 2) * 64:(g % 2) * 64 + 64, g // 2, :], Wd["b_sp"][l, g].partition_broadcast(64))
                     for g in range(4)]
                  + [(wspf[:, g, :], Wd["w_sp"][l, g]) for g in range(4)], f'lp{l}_2', writes=['lngb', 'Bsp', 'wspf'])
            for g_ in range(4):
                bank = 2 + g_ % 2
                P.op('pe', lambda e, g_=g_, bank=bank: e.transpose(PS[bank][:, 0:128], wspf[:, g_, :], C["identf"][:]),
                     reads=['wspf', 'const'], writes=[('ps', bank)])
                P.op('dve', lambda g, g_=g_, bank=bank: g.tensor_tensor(out=WT[:, g_, :], in0=PS[bank][:, 0:128],
                                                                        in1=C["trilT"][:], op=ALU.mult),
                     reads=[('ps', bank), 'const'], writes=['WT'])
            s, wv = load_w(l, [(C_DV, 256, 0)])

            def ev_v(i, bank):
                k = i % 4
                P.op('dve', lambda g: g.bn_stats(out=st6, in_=PS[bank][:, 0:256]), reads=[('ps', bank)], writes=['st6'])
                P.op('dve', lambda g: g.bn_aggr(out=mv, in_=st6), reads=['st6'], writes=['mv'])
                P.op('dve', lambda g: g.tensor_scalar(out=rstd, in0=mv[:, 1:2], scalar1=LN_EPS, scalar2=None, op0=ALU.add),
                     reads=['mv'], writes=['rstd'])
                P.op('act', lambda g: g.activation(out=rstd2, in_=rstd, func=AF.Sqrt), reads=['rstd'], writes=['rstd2'])
                P.op('dve', lambda g: g.reciprocal(out=rstd, in_=rstd2), reads=['rstd2'], writes=['rstd'])
                P.op('dve', lambda g: g.tensor_scalar(out=tmpv[k], in0=PS[bank][:, 0:256], scalar1=mv[:, 0:1],
                                                      scalar2=rstd, op0=ALU.subtract, op1=ALU.mult),
                     reads=[('ps', bank), 'mv', 'rstd'], writes=[('tmpv', k)])
                P.op('pool', lambda g: g.tensor_tensor(out=tmpv[k], in0=tmpv[k], in1=lngb[:, 0, :], op=ALU.mult),
                     reads=[('tmpv', k), 'lngb'], writes=[('tmpv', k)])
                P.op('pool', lambda g: g.tensor_tensor(out=vn[:, i, :], in0=tmpv[k], in1=lngb[:, 1, :], op=ALU.add),
                     reads=[('tmpv', k), 'lngb'], writes=[('vn', i)])
            proj_tm(s, wv, 256, ev_v)
            s2, wv2 = load_w(l, [(C_DZ, 256, 0)])
            proj_fm(s2, wv2, 2, lambda ch, tb, bank: P.op(
                'act', lambda g: g.activation(out=ug[:, ch, tb * 512:(tb + 1) * 512], in_=PS[bank][:, 0:512], func=AF.Silu),
                reads=[('ps', bank)], writes=[('ug', ch, tb)]))
            s3, wv3 = load_w(l, [(C_DU, 256, 0)])
            proj_fm(s3, wv3, 2, lambda ch, tb, bank: P.op(
                'dve', lambda g: g.tensor_tensor(out=ug[:, ch, tb * 512:(tb + 1) * 512], in0=PS[bank][:, 0:512],
                                                 in1=ug[:, ch, tb * 512:(tb + 1) * 512], op=ALU.mult),
                reads=[('ps', bank), ('ug', ch, tb)], writes=[('ug', ch, tb)]))
            for i in range(NT):
                bank = 2 + i % 2
                k = i % 4
                zt = PS[bank][:, 0:256].rearrange("p (a b) -> p a b", a=2)

                def mmz(e, i=i, zt=zt):
                    for g_ in range(4):
                        last = e.matmul(zt[(g_ % 2) * 64:(g_ % 2) * 64 + 64, g_ // 2, :],
                                        lhsT=vn[:, i, g_ * 64:(g_ + 1) * 64], rhs=WT[:, g_, :], start=True, stop=True)
                    return last
                P.op('pe', mmz, reads=[('vn', i), 'WT'], writes=[('ps', bank)])
                P.op('dve', lambda g, zt=zt, k=k: g.tensor_tensor(out=tmpz[k], in0=zt, in1=Bsp[:], op=ALU.add),
                     reads=[('ps', bank), 'Bsp'], writes=[('tmpz', k)])
                tb = i // 4
                P.op('pool', lambda g, i=i, k=k: g.tensor_tensor(out=yT[:, 6:8, i * 128:(i + 1) * 128], in0=tmpz[k],
                                                                 in1=ug[:, :, i * 128:(i + 1) * 128], op=ALU.mult),
                     reads=[('tmpz', k), ('ug', 0, tb), ('ug', 1, tb)], writes=[('yT', 6)])
            P.barrier()

        if "B" in phases:
            QBd = carve(0, [128, 2, 2, 2048], BF16)
            KB = carve(16384, [128, 2, 2048], BF16)
            szB = carve(24576, [128, 2, 2048], BF16)
            VB = carve(32768, [128, 16, 4, 65], BF16)
            PTP = [carve(41088 + k * 2048, [128, 1024], BF16) for k in range(3)]
            obB = [carve(47232 + k * 512, [128, 4, 64], BF16) for k in range(2)]
            rden = small[:, 80:84]
            P.op('dve', lambda g: g.memset(VB[:, :, :, 64:65], 1.0), writes=['VBones'])
            P.op('dve', lambda g: g.memset(QBd[0:64, :, 1, :], 0.0), writes=['QBz'])
            P.op('dve', lambda g: g.memset(QBd[64:128, :, 0, :], 0.0), writes=['QBz'])
            s, wv = load_w(l, [(C_BV, 256, 0)])
            proj_tm(s, wv, 256, lambda i, bank: copy_op(
                alt(), VB[:, i, :, 0:64], PS[bank][:, 0:256].rearrange("p (h d) -> p h d", h=4),
                reads=[('ps', bank)], writes=[('VB', i)]))
            s2, wv2 = load_w(l, [(C_BZ, 256, 0)])
            proj_fm(s2, wv2, 2, lambda ch, tb, bank: P.op(
                'act', lambda g: g.activation(out=szB[:, ch, tb * 512:(tb + 1) * 512], in_=PS[bank][:, 0:512], func=AF.Silu),
                reads=[('ps', bank)], writes=['szB']))
            s3, wv3 = load_w(l, [(C_BQ, 256, 0)])

            def ev_q(ch, tb, bank):
                copy_op('dve', QBd[0:64, ch, 0, tb * 512:(tb + 1) * 512], PS[bank][0:64, 0:512],
                        reads=[('ps', bank), 'QBz'], writes=['QB'])
                copy_op('act', QBd[64:128, ch, 1, tb * 512:(tb + 1) * 512], PS[bank][64:128, 0:512],
                        reads=[('ps', bank), 'QBz'], writes=['QB'])
            proj_fm(s3, wv3, 2, ev_q)
            s4, wv4 = load_w(l, [(C_BK, 256, 0)])
            proj_fm(s4, wv4, 2, lambda ch, tb, bank: copy_op(
                alt(), KB[:, ch, tb * 512:(tb + 1) * 512], PS[bank][:, 0:512], reads=[('ps', bank)], writes=['KB']))

            P.op('pe', lambda e: e.transpose(psb(0)[:, 0:128], C["identb"][:], C["identb"][:]), reads=['const'],
                 writes=[('ps', 0), ('ps0h', 0), ('ps0h', 1)])
            units = []
            for qt in range(NT):
                for kt in range(qt + 1):
                    units.append(dict(qt=qt, kt=kt, first=(kt == 0), last=(kt == qt)))
            LAG = 2
            npairs = (len(units) + 1) // 2
            deferred = []

            def run_deferred_b(step, force=False):
                rest = []
                for (due, fn) in deferred:
                    if force or due <= step:
                        fn()
                    else:
                        rest.append((due, fn))
                deferred[:] = rest
            for step in range(npairs + LAG + 4):
                run_deferred_b(step)
                for us_ in (2 * step, 2 * step + 1):
                    if us_ >= len(units):
                        continue
                    u = units[us_]
                    sbank = 2 + 2 * (step % 3) + (us_ % 2)
                    u['sbank'] = sbank
                    qt, kt = u['qt'], u['kt']

                    def mms(e, sbank=sbank, qt=qt, kt=kt):
                        for c in range(2):
                            last = e.matmul(PS[sbank][:, c * 256:(c + 1) * 256], lhsT=KB[:, c, kt * 128:(kt + 1) * 128],
                                            rhs=QBd[:, c, :, qt * 128:(qt + 1) * 128], start=True, stop=True)
                        return last
                    P.op('pe', mms, reads=['QB', 'QBz', 'KB'], writes=[('ps', sbank)])
                pq = step - LAG
                if pq < 0 or pq >= npairs:
                    continue
                js = [j_ for j_ in (2 * pq, 2 * pq + 1) if j_ < len(units)]
                slot = pq % 3
                nj = len(js)
                P.op('act', lambda g, slot=slot, nj=nj: g.activation(
                    out=PTP[slot][:, 0:nj * 512], in_=psall[:, (2 + 2 * slot) * 512:(2 + 2 * slot + nj) * 512],
                    func=AF.Exp, scale=0.125),
                    reads=[('ps', 2 + 2 * slot + i_) for i_ in range(nj)], writes=[('pt', slot, i_) for i_ in range(nj)])
                for us in js:
                    u = units[us]
                    sbank = u['sbank']
                    pt = PTP[slot][:, (us % 2) * 512:(us % 2 + 1) * 512]
                    ptk = ('pt', slot, us % 2)
                    qt, kt = u['qt'], u['kt']
                    e0 = kt - qt + 15
                    P.op('dve', lambda g, pt=pt, e0=e0: g.tensor_tensor(
                        out=pt[:, 0:512].rearrange("p (j q) -> p j q", j=4),
                        in0=pt[:, 0:512].rearrange("p (j q) -> p j q", j=4),
                        in1=C["mtab"][:, e0, :].unsqueeze(1).broadcast_to([128, 4, 128]), op=ALU.mult),
                        reads=[ptk, 'const'], writes=[ptk])
                    ob = 1
                    ov = PS[ob][:, 0:260].rearrange("p (j d) -> p j d", j=4)

                    def mmo(e, u=u, pt=pt, ov=ov, kt=kt):
                        for j in range(4):
                            last = e.matmul(ov[:, j, :], lhsT=pt[:, j * 128:(j + 1) * 128], rhs=VB[:, kt, j, :],
                                            start=(u['first'] and j == 0), stop=True, skip_group_check=True)
                        return last
                    P.op('pe', mmo, reads=[ptk, ('VB', kt), 'VBones'], writes=[('ps', ob)])
                    if u['last']:
                        k = qt % 2
                        P.op('dve', lambda g, ov=ov: g.reciprocal(out=rden.unsqueeze(2), in_=ov[:, :, 64:65]),
                             reads=[('ps', ob)], writes=['rden'])
                        P.op('dve', lambda g, ov=ov, k=k: g.tensor_tensor(
                            out=obB[k], in0=ov[:, :, 0:64], in1=rden.unsqueeze(2).broadcast_to([128, 4, 64]), op=ALU.mult),
                            reads=[('ps', ob), 'rden'], writes=[('obB', k)])

                        def fin_b(k=k, qt=qt):
                            h0 = (qt % 2) * 512

                            def trb(e):
                                for c in range(2):
                                    last = e.transpose(psb(0)[:, h0 + c * 128:h0 + (c + 1) * 128],
                                                       obB[k][:, 2 * c:2 * c + 2, :].rearrange("p a b -> p (a b)"), C["identb"][:])
                                return last
                            P.op('pe', trb, reads=[('obB', k), 'const'], writes=[('ps0h', qt % 2)])
                            P.op('dve', lambda g: g.tensor_tensor(
                                out=yT[:, 2:4, qt * 128:(qt + 1) * 128],
                                in0=psb(0)[:, h0:h0 + 256].rearrange("p (c q) -> p c q", c=2),
                                in1=szB[:, :, qt * 128:(qt + 1) * 128], op=ALU.mult),
                                reads=[('ps0h', qt % 2), 'szB'], writes=[('yT', 2)])
                        deferred.append((step + 2, fin_b))
            run_deferred_b(0, force=True)
            P.barrier()

        if "A" in phases or "a" in phases:
            kc2 = carve(0, [128, 2, 2048], F32)
            G = carve(16384, [128, 2, 16, 128], BF16)
            hid = carve(24576, [128, 2, 2, 128], BF16)
            w2f = carve(25600, [128, 2, 2, 64], F32)
            with nc.allow_non_contiguous_dma(reason="tiny pe transposes"):
                P.dma('sp', [(pe2[:, c, :], Wd["pe_cmp"][l, c].rearrange("(jp par) d -> (par d) jp", par=2))
                             for c in range(2)]
                      + [(w2f[:, c, :, :], Wd["w_cmp2"][l, c].rearrange("(hh p) d -> p hh d", p=128)) for c in range(2)],
                      f'lp{l}_3', writes=['pe2', 'w2f'])
            P.op('dve', lambda g: g.tensor_copy(out=W2k[:, :, 0:64], in_=w2f[:, 0, :, :]), reads=['w2f'], writes=['W2k'])
            P.op('dve', lambda g: g.tensor_copy(out=W2v[:, :, :], in_=w2f[:, 1, :, :]), reads=['w2f'], writes=['W2v'])
            P.op('dve', lambda g: g.tensor_copy(out=vcx[:, 64:97], in_=C["ovl"][:]), reads=['const'], writes=['vcx'])
            s, wv = load_w(l, [(C_KC, 64, 0), (C_KC, 64, 64), (C_VC, 64, 128), (C_VC, 64, 192)])

            def ev_kc(ch, tb, bank):
                copy_op(alt(), kc2[0:64, ch, tb * 512:(tb + 1) * 512], PS[bank][0:64, 0:512],
                        reads=[('ps', bank)], writes=['kc2'])
                if tb == 0:
                    copy_op(alt(), kc2[64:128, ch, 0:511], PS[bank][64:128, 1:512], reads=[('ps', bank)], writes=['kc2'])
                else:
                    copy_op(alt(), kc2[64:128, ch, tb * 512 - 1:(tb + 1) * 512 - 1], PS[bank][64:128, 0:512],
                            reads=[('ps', bank)], writes=['kc2'])
            proj_fm(s, wv, 2, ev_kc)
            for c in range(2):
                for jp in range(16):
                    P.op('dve', lambda g, c=c, jp=jp: g.tensor_scalar(
                        out=G[:, c, jp, 0:127], in0=kc2[:, c, 2 * jp:2 * jp + 16 * 126 + 1:16],
                        scalar1=pe2[:, c, jp:jp + 1], scalar2=None, op0=ALU.add),
                        reads=['kc2', 'pe2'], writes=[('G', c)])
            for c in range(2):
                for hh in range(2):
                    sl, w1v = next_w(l, 'w1', (c, hh))
                    bank = pbank['n'] % 2
                    pbank['n'] += 1

                    def mmh(e, c=c, w1v=w1v, bank=bank):
                        for jp in range(16):
                            last = e.matmul(PS[bank][:, 0:127], lhsT=w1v[:, jp, :], rhs=G[:, c, jp, 0:127],
                                            start=(jp == 0), stop=(jp == 15))
                        return last
                    P.op('pe', mmh, reads=[('w', sl), ('G', c)], writes=[('ps', bank)])
                    P.op('act', lambda g, c=c, hh=hh, bank=bank: g.activation(
                        out=hid[:, c, hh, 0:127], in_=PS[bank][:, 0:127], func=AF.Gelu_apprx_tanh),
                        reads=[('ps', bank)], writes=[('hid', c)])
            bank = pbank['n'] % 2
            pbank['n'] += 1

            def mmk(e, bank=bank):
                for hh in range(2):
                    last = e.matmul(PS[bank][:, 0:127], lhsT=W2k[:, hh, :], rhs=hid[:, 0, hh, 0:127],
                                    start=(hh == 0), stop=(hh == 1))
                return last
            P.op('pe', mmk, reads=['W2k', ('hid', 0)], writes=[('ps', bank)])
            P.op('dve', lambda g, bank=bank: g.tensor_copy(out=kcT[:, 0:127], in_=PS[bank][:, 0:127]),
                 reads=[('ps', bank)], writes=['kcT'])
            bank = pbank['n'] % 2
            pbank['n'] += 1

            def mmv(e, bank=bank):
                for hh in range(2):
                    last = e.matmul(PS[bank][0:127, 0:64], lhsT=hid[:, 1, hh, 0:127], rhs=W2v[:, hh, :],
                                    start=(hh == 0), stop=(hh == 1))
                return last
            P.op('pe', mmv, reads=['W2v', ('hid', 1)], writes=[('ps', bank)])
            P.op('dve', lambda g, bank=bank: g.tensor_copy(out=vcx[0:127, 0:64], in_=PS[bank][0:127, 0:64]),
                 reads=[('ps', bank)], writes=['vcx'])
            P.barrier()

            if "A" in phases:
                QS = carve(0, [128, 4, 2048], BF16)
                KS = carve(16384, [128, 2048], BF16)
                KW = carve(20480, [128, 2048], BF16)
                szA = carve(24576, [128, 2, 2048], BF16)
                VA = carve(32768, [128, 2, 16, 65], BF16)
                PT = [carve(36928 + k * 1024, [128, 512], BF16) for k in range(2)] + [carve(44608, [128, 512], BF16), carve(46912, [128, 512], BF16)]
                obA = [carve(38976 + k * 512, [128, 4, 64], BF16) for k in range(2)]
                sel2 = [carve(40000 + k * 256, [128, 2, 64], BF16) for k in range(2)]
                ocmp = [carve(40512 + k * 1024, [128, 4, 64], F32) for k in range(2)] + [carve(45632, [128, 4, 64], F32)]
                t1 = carve(42560, [128, 4, 64], F32)
                t2 = carve(43584, [128, 4, 64], F32)
                imp = [sm2[:, 0:32], sm2[:, 32:64]]
                impf = sm2[:, 64:96]
                tmp32 = sm2[:, 96:128]
                m8a = sm2[:, 128:136]
                m8b = sm2[:, 136:144]
                dn = sm2[:, 144:148]
                rdc = sm2[:, 148:152]
                cf1 = sm2[:, 152:156]
                rd2 = sm2[:, 156:160]
                rd3 = sm2[:, 160:164]
                cf2 = sm2[:, 164:168]
                cf3 = sm2[:, 168:172]
                P.op('dve', lambda g: g.memset(VA[:, :, :, 64:65], 1.0), writes=['VAones'])
                for k in range(2):
                    P.op('dve', lambda g, k=k: g.memset(sel2[k][:, :, 32:64], 1.0), writes=[('sel2pad', k)])
                P.dma('sp', [(KS[64:128, :], Cd["efull"])], f'lp{l}_4', writes=['KSe'])
                P.op('dve', lambda g: g.memset(KW[64:128, :], 0.0), writes=['KWz'])
                P.op('dve', lambda g: g.memset(QS[64:128, :, :], 0.0), writes=[('biasT', q_) for q_ in range(NT)])
                s, wv = load_w(l, [(C_VS, 64, 0), (C_VW, 64, 64), (C_AG, 12, 128)])

                def ev_ta(i, bank):
                    copy_op('dve', VA[:, :, i, 0:64], PS[bank][:, 0:128].rearrange("p (a d) -> p a d", a=2),
                            reads=[('ps', bank)], writes=[('VA', i)])
                    P.op('act', lambda g: g.activation(out=sg[:, i, :], in_=PS[bank][:, 128:140], func=AF.Sigmoid),
                         reads=[('ps', bank)], writes=[('sg', i)])
                proj_tm(s, wv, 140, ev_ta)
                s2, wv2 = load_w(l, [(C_AZ, 256, 0)])
                proj_fm(s2, wv2, 2, lambda ch, tb, bank: P.op(
                    'act', lambda g: g.activation(out=szA[:, ch, tb * 512:(tb + 1) * 512], in_=PS[bank][:, 0:512], func=AF.Silu),
                    reads=[('ps', bank)], writes=['szA']))
                s3, wv3 = load_w(l, [(C_AQ, 256, 0)])
                proj_fm(s3, wv3, 2, lambda ch, tb, bank: copy_op(
                    alt(), QS[0:64, 2 * ch, tb * 512:(tb + 1) * 512], PS[bank][0:64, 0:512], reads=[('ps', bank)], writes=['QS']))
                s3b, wv3b = load_w(l, [(C_AQ + 64, 64, 0), (C_AQ, 64, 64), (C_AQ + 192, 64, 128), (C_AQ + 128, 64, 192)])
                proj_fm(s3b, wv3b, 2, lambda ch, tb, bank: copy_op(
                    alt(), QS[0:64, 2 * ch + 1, tb * 512:(tb + 1) * 512], PS[bank][0:64, 0:512], reads=[('ps', bank)], writes=['QS']))
                s4, wv4 = load_w(l, [(C_KS, 64, 0), (C_KS, 64, 64), (C_KW, 64, 128), (C_KW, 64, 192)])
                proj_fm(s4, wv4, 2, lambda ch, tb, bank: copy_op(
                    alt(), (KS if ch == 0 else KW)[0:64, tb * 512:(tb + 1) * 512], PS[bank][0:64, 0:512],
                    reads=[('ps', bank)], writes=['KSW']))

                units = []
                units.append(dict(kind='cmp', qt=0, kt=0))
                units.append(dict(kind='cmp', qt=1, kt=0))
                for qt in range(NT):
                    if qt + 2 < NT:
                        units.append(dict(kind='cmp', qt=qt + 2, kt=0))
                    k0 = max(0, qt - 4)
                    for kt in range(k0, qt + 1):
                        units.append(dict(kind='win', qt=qt, kt=kt, first=(kt == k0), last=(kt == qt)))
                    for kt in range(qt + 1):
                        units.append(dict(kind='slc', qt=qt, kt=kt, first=(kt == 0), last=(kt == qt)))
                import os
                units = units[:int(os.environ.get('A_LIM', '100000'))]
                bias_done = set()
                deferred = []
                ccount = {'slc': 0, 'k64': 0}

                def run_deferred(step, force=False):
                    rest = []
                    for (due, fn) in deferred:
                        if force or due <= step:
                            fn()
                        else:
                            rest.append((due, fn))
                    deferred[:] = rest

                LAG = 3
                for step in range(len(units) + LAG + 12):
                    run_deferred(step)
                    if step < len(units):
                        u = units[step]
                        qt, kt, kind = u['qt'], u['kt'], u['kind']
                        if kind == 'slc' and qt not in bias_done:
                            run_deferred(step, force=True)
                            assert qt in bias_done
                        u['sbank'] = 4 + step % 4
                        u['ptk'] = step % 4
                        sbank = u['sbank']
                        qs = slice(qt * 128, (qt + 1) * 128)
                        if kind == 'cmp':
                            def mmc(e, sbank=sbank, qs=qs, qt=qt):
                                e.matmul(PS[sbank][:, 0:512], lhsT=kcT[:, 0:128], rhs=QS[:, :, qs], start=True, stop=False)
                                return e.matmul(PS[sbank][:, 0:512], lhsT=C["identb"][:],
                                                rhs=C["mc"][:, qt, :].unsqueeze(1).broadcast_to([128, 4, 128]),
                                                start=False, stop=True)
                            P.op('pe', mmc, reads=['QS', 'kcT', ('biasT', qt), 'const'], writes=[('ps', sbank)])
                        elif kind == 'win':
                            mki = 0 if kt == qt else (1 if kt == qt - 4 else None)

                            def mmw(e, sbank=sbank, qs=qs, kt=kt, mki=mki):
                                last = e.matmul(PS[sbank][:, 0:512], lhsT=KW[:, kt * 128:(kt + 1) * 128], rhs=QS[:, :, qs],
                                                start=True, stop=(mki is None))
                                if mki is not None:
                                    last = e.matmul(PS[sbank][:, 0:512], lhsT=C["identb"][:],
                                                    rhs=C["tri"][:, mki, :].unsqueeze(1).broadcast_to([128, 4, 128]),
                                                    start=False, stop=True)
                                return last
                            P.op('pe', mmw, reads=['QS', 'KSW', 'KWz', ('biasT', qt), 'const'], writes=[('ps', sbank)])
                        else:
                            def mmsl(e, sbank=sbank, qs=qs, kt=kt, dg=(kt == qt)):
                                last = e.matmul(PS[sbank][:, 0:512], lhsT=KS[:, kt * 128:(kt + 1) * 128], rhs=QS[:, :, qs],
                                                start=True, stop=(not dg))
                                if dg:
                                    last = e.matmul(PS[sbank][:, 0:512], lhsT=C["identb"][:],
                                                    rhs=C["tri"][:, 0, :].unsqueeze(1).broadcast_to([128, 4, 128]),
                                                    start=False, stop=True)
                                return last
                            P.op('pe', mmsl, reads=['QS', 'KSW', 'KSe', ('biasT', qt), 'const'], writes=[('ps', sbank)])
                    us = step - LAG
                    if us < 0 or us >= len(units):
                        continue
                    u = units[us]
                    qt, kt, kind = u['qt'], u['kt'], u['kind']
                    sbank = u['sbank']
                    pt = PT[u['ptk']]
                    ptk = ('pt', u['ptk'])
                    k = qt % 2
                    P.op('act', lambda g, sbank=sbank, pt=pt: g.activation(out=pt[:, 0:512], in_=PS[sbank][:, 0:512],
                                                                         func=AF.Exp, scale=0.125),
                         reads=[('ps', sbank)], writes=[ptk])
                    if kind == 'cmp':
                        U = PS[1][:, 0:388].rearrange("p (h d) -> p h d", h=4)

                        def mmu(e, pt=pt, U=U):
                            for h in range(4):
                                last = e.matmul(U[:, h, :], lhsT=pt[:, h * 128:(h + 1) * 128], rhs=vcx[:, :],
                                                start=True, stop=True)
                            return last
                        P.op('pe', mmu, reads=[ptk, 'vcx'], writes=['psU'])
                        P.op('dve', lambda g, U=U: g.tensor_scalar(out=dn.unsqueeze(2), in0=U[:, :, 64:65], scalar1=1e-30,
                                                                   scalar2=None, op0=ALU.max), reads=['psU'], writes=['dn'])
                        P.op('dve', lambda g: g.reciprocal(out=rdc, in_=dn), reads=['dn'], writes=['rdc'])
                        P.op('dve', lambda g, qt=qt: g.tensor_tensor(out=cf1, in0=rdc, in1=sg[:, qt, 0:4], op=ALU.mult),
                             reads=['rdc', ('sg', qt)], writes=['cf1'])
                        P.op('dve', lambda g, U=U, k=k: g.tensor_tensor(
                            out=ocmp[qt % 3], in0=U[:, :, 0:64], in1=cf1.unsqueeze(2).broadcast_to([128, 4, 64]), op=ALU.mult),
                            reads=['psU', 'cf1'], writes=[('ocmp', qt % 3)])
                        for h in range(4):
                            if h == 0:
                                P.op('dve', lambda g, U=U, k=k: g.tensor_scalar(out=imp[k], in0=U[:, 0, 65:97], scalar1=rdc[:, 0:1],
                                                                               scalar2=None, op0=ALU.mult),
                                     reads=['psU', 'rdc'], writes=[('imp', k)])
                            else:
                                P.op('dve', lambda g, U=U, k=k, h=h: g.scalar_tensor_tensor(
                                    out=imp[k], in0=U[:, h, 65:97], scalar=rdc[:, h:h + 1], in1=imp[k], op0=ALU.mult, op1=ALU.add),
                                    reads=['psU', 'rdc', ('imp', k)], writes=[('imp', k)])
                        P.op('dve', lambda g, k=k, qt=qt: g.tensor_tensor(out=impf, in0=imp[k], in1=C["tka"][:, qt, :], op=ALU.mult),
                             reads=[('imp', k), 'const'], writes=['impf'])
                        P.op('dve', lambda g, qt=qt: g.tensor_tensor(out=impf, in0=impf, in1=C["tkb"][:, qt, :], op=ALU.add),
                             reads=['impf', 'const'], writes=['impf'])
                        P.op('dve', lambda g: g.max(out=m8a, in_=impf), reads=['impf'], writes=['m8a'])
                        P.op('dve', lambda g: g.match_replace(out=tmp32, in_to_replace=m8a, in_values=impf, imm_value=-1e30),
                             reads=['m8a', 'impf'], writes=['tmp32'])
                        P.op('dve', lambda g: g.max(out=m8b, in_=tmp32), reads=['tmp32'], writes=['m8b'])
                        P.op('dve', lambda g, k=k: g.tensor_scalar(
                            out=sel2[k][:, :, 0:32], in0=impf.unsqueeze(1).broadcast_to([128, 2, 32]),
                            scalar1=m8b[:, 7:8], scalar2=None, op0=ALU.is_ge),
                            reads=['impf', 'm8b'], writes=[('sel2', k)])

                        def fin_bias(k=k, qt=qt):
                            P.op('pe', lambda e: e.transpose(psb(0)[:, 0:128], sel2[k][:, :, :].rearrange("p a b -> p (a b)"),
                                                             C["identb"][:]),
                                 reads=[('sel2', k), ('sel2pad', k), 'const'], writes=['ps0a'])
                            P.op('dve', lambda g: g.tensor_scalar(
                                out=QS[64:128, :, qt * 128:(qt + 1) * 128],
                                in0=psb(0)[64:128, 0:128].unsqueeze(1).broadcast_to([64, 4, 128]),
                                scalar1=-1.0, scalar2=-BIGNEG, op0=ALU.add, op1=ALU.mult),
                                reads=['ps0a'], writes=[('biasT', qt)])
                            bias_done.add(qt)
                        deferred.append((step + 11, fin_bias))
                    else:
                        br = 0 if kind == 'slc' else 1
                        ob = 2 + br
                        ov = PS[ob][:, 0:260].rearrange("p (h d) -> p h d", h=4)

                        def mmo(e, u=u, pt=pt, ov=ov, br=br, kt=kt):
                            for h in range(4):
                                last = e.matmul(ov[:, h, :], lhsT=pt[:, h * 128:(h + 1) * 128], rhs=VA[:, br, kt, :],
                                                start=(u['first'] and h == 0), stop=True, skip_group_check=True)
                            return last
                        P.op('pe', mmo, reads=[ptk, 'VAones', ('VA', kt)], writes=[('ps', ob)])
                        if u['last'] and kind == 'slc':
                            ovs = PS[2][:, 0:260].rearrange("p (h d) -> p h d", h=4)
                            ovw = PS[3][:, 0:260].rearrange("p (h d) -> p h d", h=4)
                            P.op('dve', lambda g, ovs=ovs: g.reciprocal(out=rd2.unsqueeze(2), in_=ovs[:, :, 64:65]),
                                 reads=[('ps', 2)], writes=['rd2'])
                            P.op('dve', lambda g, ovw=ovw: g.reciprocal(out=rd3.unsqueeze(2), in_=ovw[:, :, 64:65]),
                                 reads=[('ps', 3)], writes=['rd3'])
                            P.op('dve', lambda g, qt=qt: g.tensor_tensor(out=cf2, in0=rd2, in1=sg[:, qt, 4:8], op=ALU.mult),
                                 reads=['rd2', ('sg', qt)], writes=['cf2'])
                            P.op('dve', lambda g, qt=qt: g.tensor_tensor(out=cf3, in0=rd3, in1=sg[:, qt, 8:12], op=ALU.mult),
                                 reads=['rd3', ('sg', qt)], writes=['cf3'])
                            P.op('dve', lambda g, ovs=ovs: g.tensor_tensor(
                                out=t1, in0=ovs[:, :, 0:64], in1=cf2.unsqueeze(2).broadcast_to([128, 4, 64]), op=ALU.mult),
                                reads=[('ps', 2), 'cf2'], writes=['t1'])
                            P.op('dve', lambda g, ovw=ovw: g.tensor_tensor(
                                out=t2, in0=ovw[:, :, 0:64], in1=cf3.unsqueeze(2).broadcast_to([128, 4, 64]), op=ALU.mult),
                                reads=[('ps', 3), 'cf3'], writes=['t2'])
                            P.op('dve', lambda g, qt=qt: g.tensor_tensor(out=t1, in0=t1, in1=ocmp[qt % 3], op=ALU.add),
                                 reads=['t1', ('ocmp', qt % 3)], writes=['t1'])
                            P.op('dve', lambda g, k=k: g.tensor_tensor(out=obA[k], in0=t1, in1=t2, op=ALU.add),
                                 reads=['t1', 't2'], writes=[('obA', k)])

                            def fin_out(k=k, qt=qt):
                                def tra(e):
                                    for c in range(2):
                                        last = e.transpose(psb(0)[:, 256 + c * 128:256 + (c + 1) * 128],
                                                           obA[k][:, 2 * c:2 * c + 2, :].rearrange("p a b -> p (a b)"), C["identb"][:])
                                    return last
                                P.op('pe', tra, reads=[('obA', k), 'const'], writes=['ps0b'])
                                P.op('dve', lambda g: g.tensor_tensor(
                                    out=yT[:, 0:2, qt * 128:(qt + 1) * 128],
                                    in0=psb(0)[:, 256:512].rearrange("p (c q) -> p c q", c=2),
                                    in1=szA[:, :, qt * 128:(qt + 1) * 128], op=ALU.mult),
                                    reads=['ps0b', 'szA'], writes=[('yT', 0)])
                            deferred.append((step + 6, fin_out))
                run_deferred(0, force=True)
                P.barrier()

        if dbg and l == 0:
            P.barrier()
            P.dma('sp', [(dbg_y, yT[:])], 'dbg', reads=[])
            P.eng['sp'].wait_ge(P.dsem['dbg'][0], P.dsem['dbg'][1])
            P.barrier()

        wout = hT[:, 0:4, :].rearrange("p a b -> p (a b)").rearrange("p (kc n) -> p kc n", kc=8)
        gpb = hT[:, 4:6, :].rearrange("p a b -> p (a b)").bitcast(F32)[:, 0:1024]
        P.dma('pool', [(wout[:, :, nb * 512:(nb + 1) * 512],
                        Wd["w_out"][l][:, nb * 512:(nb + 1) * 512].rearrange("(kc p) c -> p kc c", p=128))
                       for nb in range(2)], 'wout', writes=['wout', 'hT'])
        P.dma('sp', [(gpb, Wd["g_post"][l].partition_broadcast(128))], f'lp{l}_5', writes=['gpb'])
        tmpo = [carve(k * 4096, [128, 1024], F32) for k in range(2)]
        junk2 = carve(8192, [128, 512], BF16)
        ss2 = small[:, 96:98]
        ss3 = small[:, 98:99]
        ss4 = small[:, 99:100]
        for i in range(NT):
            banks = [(i % 4) * 2, (i % 4) * 2 + 1]
            k = i % 2
            for nb in range(2):
                def mmo2(e, i=i, nb=nb, bank=banks[nb]):
                    for kc in range(8):
                        last = e.matmul(PS[bank][:, 0:512], lhsT=yT[:, kc, i * 128:(i + 1) * 128],
                                        rhs=wout[:, kc, nb * 512:(nb + 1) * 512], start=(kc == 0), stop=(kc == 7))
                    return last
                P.op('pe', mmo2, reads=['wout'] + [('yT', c) for c in range(8)], writes=[('ps', banks[nb])])
                P.op('act', lambda g, nb=nb, bank=banks[nb]: g.activation(out=junk2, in_=PS[bank][:, 0:512], func=AF.Square,
                                                                          accum_out=ss2[:, nb:nb + 1]),
                     reads=[('ps', banks[nb])], writes=['junk2', ('ss2', nb)])
            P.op('dve', lambda g: g.tensor_tensor(out=ss3, in0=ss2[:, 0:1], in1=ss2[:, 1:2], op=ALU.add),
                 reads=[('ss2', 0), ('ss2', 1)], writes=['ss3'])
            P.op('dve', lambda g: g.tensor_scalar(out=ss3, in0=ss3, scalar1=1.0 / D, scalar2=RMS_EPS, op0=ALU.mult, op1=ALU.add),
                 reads=['ss3'], writes=['ss3'])
            P.op('act', lambda g: g.activation(out=ss4, in_=ss3, func=AF.Sqrt), reads=['ss3'], writes=['ss4'])
            P.op('dve', lambda g: g.reciprocal(out=ss3, in_=ss4), reads=['ss4'], writes=['ss3'])
            for nb in range(2):
                P.op('dve', lambda g, nb=nb, k=k, bank=banks[nb]: g.scalar_tensor_tensor(
                    out=tmpo[k][:, nb * 512:(nb + 1) * 512], in0=PS[bank][:, 0:512], scalar=ss3,
                    in1=gpb[:, nb * 512:(nb + 1) * 512], op0=ALU.mult, op1=ALU.mult),
                    reads=[('ps', banks[nb]), 'ss3', 'gpb'], writes=[('tmpo', k)])
            P.op('dve', lambda g, i=i, k=k: g.tensor_tensor(out=x_sb[:, i, :], in0=x_sb[:, i, :], in1=tmpo[k], op=ALU.add),
                 reads=[('tmpo', k), ('x', i)], writes=[('x', i)])
            if l == n_layers - 1:
                P.dma('sp', [(out_d[i * 128:(i + 1) * 128, :], x_sb[:, i, :])], 'out', reads=[('x', i)])
        P.barrier()

    P.eng['sp'].wait_ge(P.dsem['out'][0], P.dsem['out'][1])
    print('PROG counts', P.cnt, {k: v[1] for k, v in P.dsem.items()})
    return nc


_CACHE = {}


def _get_prog(n_layers, phases="CDBA", dbg=False):
    key = (n_layers, phases, dbg)
    if key not in _CACHE:
        _CACHE[key] = build(n_layers, phases, dbg)
    return _CACHE[key]


def kernel(x, g_pre, w_in, pe_cmp, w_cmp1, w_cmp2, w_pool, pool_scale, sg_ln_g, sg_ln_b, w_sp, b_sp, w_out, g_post):
    ws = dict(g_pre=g_pre, w_in=w_in, pe_cmp=pe_cmp, w_cmp1=w_cmp1, w_cmp2=w_cmp2, w_pool=w_pool,
              pool_scale=pool_scale, sg_ln_g=sg_ln_g, sg_ln_b=sg_ln_b, w_sp=w_sp, b_sp=b_sp, w_out=w_out, g_post=g_post)
    ws = {k: np.ascontiguousarray(np.asarray(v, dtype=np.float32)) for k, v in ws.items()}
    x = np.ascontiguousarray(np.asarray(x, dtype=np.float32))
    consts = {"c_" + k: v for k, v in host_consts().items()}
    nc = _get_prog(DEPTH)
    in_maps = []
    for b in range(NCORES):
        m = {"x": x[b]}
        m.update(ws)
        m.update(consts)
        in_maps.append(m)
    res = run_bass_kernel_spmd(nc, in_maps, core_ids=list(range(NCORES)))
    return np.stack([res.results[b]["out"] for b in range(NCORES)], axis=0).astype(np.float32)
```

```python
import numpy as np
import ml_dtypes
import concourse.bass as bass
import concourse.mybir as mybir
from concourse.bass_utils import run_bass_kernel_spmd

F32 = mybir.dt.float32
BF16 = mybir.dt.bfloat16
AF = mybir.ActivationFunctionType
ALU = mybir.AluOpType

S = 2048
D = 1024
NT = 16
KC = 8
DIN = 3212
DEPTH = 4
NCORES = 8
RMS_EPS = 1e-6
LN_EPS = 1e-5
POOLS = (2, 4, 8, 16)
C_AQ, C_KC, C_VC, C_KS, C_VS, C_KW, C_VW, C_AG, C_AZ = 0, 256, 320, 384, 448, 512, 576, 640, 652
C_BQ, C_BK, C_BV, C_BZ, C_CIN, C_CZ, C_DU, C_DV, C_DZ = 908, 1164, 1420, 1676, 1932, 2188, 2444, 2700, 2956
BIGNEG = -240000.0


class Prog:
    def __init__(self, nc):
        self.nc = nc
        self.eng = dict(pe=nc.tensor, act=nc.scalar, dve=nc.vector, pool=nc.gpsimd, sp=nc.sync)
        self.sem = {e: nc.alloc_semaphore(name=f"sem_{e}") for e in self.eng}
        self.cnt = {e: 0 for e in self.eng}
        self.seen = {e: {} for e in self.eng}
        self.lw = {}
        self.rd = {}
        self.dsem = {}

    def _handle(self, sk):
        return self.sem[sk] if sk in self.sem else self.dsem[sk][0]

    def _deps(self, e, reads, writes):
        need = {}

        def add(t):
            if need.get(t[0], 0) < t[1]:
                need[t[0]] = t[1]
        for r in reads:
            t = self.lw.get(r)
            if t:
                add(t)
        for w in writes:
            t = self.lw.get(w)
            if t and t[0] != e:
                add(t)
            for sk, v in self.rd.get(w, {}).items():
                if sk != e:
                    add((sk, v))
        for sk, val in need.items():
            if self.seen[e].get(sk, 0) >= val:
                continue
            self.seen[e][sk] = val
            self.eng[e].wait_ge(self._handle(sk), val)

    def _commit(self, tok, reads, writes):
        for w in writes:
            self.lw[w] = tok
            self.rd[w] = {}
        for r in reads:
            d = self.rd.setdefault(r, {})
            if d.get(tok[0], 0) < tok[1]:
                d[tok[0]] = tok[1]

    def op(self, e, fn, reads=(), writes=()):
        self._deps(e, reads, writes)
        inst = fn(self.eng[e])
        self.cnt[e] += 1
        inst.then_inc(self.sem[e], 1)
        tok = (e, self.cnt[e])
        self._commit(tok, reads, writes)
        return tok

    def dma(self, q, pairs, semname, reads=(), writes=(), **kw):
        if semname not in self.dsem:
            self.dsem[semname] = [self.nc.alloc_semaphore(name=f"d_{semname}"), 0]
        self._deps(q, reads, writes)
        h = self.dsem[semname]
        for (o, i) in pairs:
            inst = self.eng[q].dma_start(out=o, in_=i, **kw)
            h[1] += 16
            inst.then_inc(h[0], 16)
        tok = (semname, h[1])
        self._commit(tok, reads, writes)
        return tok

    def barrier(self):
        ces = ['pe', 'act', 'dve']
        for e in ces + ['sp']:
            for e2 in ces + ['pool']:
                if e2 == e:
                    continue
                v = self.cnt[e2]
                if v > self.seen[e].get(e2, 0):
                    self.seen[e][e2] = v
                    self.eng[e].wait_ge(self.sem[e2], v)


def host_consts():
    bf = ml_dtypes.bfloat16
    c = {}
    c["identb"] = np.eye(128, dtype=np.float32).astype(bf)
    c["identf"] = np.eye(128, dtype=np.float32)
    k = np.arange(128)[:, None, None]
    e = np.arange(16)[None, :, None]
    q = np.arange(128)[None, None, :]
    d = (15 - e) * 128 + q - k
    mult = ((d <= 128).astype(np.float32) + ((d % 4 == 0) & (d <= 512)) + (d % 16 == 0)) * (d >= 0)
    c["mtab"] = mult.astype(bf)
    n = np.arange(128)[:, None, None]
    qt = np.arange(16)[None, :, None]
    mc = ((16 * n + 31 <= 128 * qt + q) & (n < 127)).astype(np.float32)
    c["mc"] = ((mc - 1.0) * 240000.0).astype(bf)
    kk = np.arange(128)[:, None]
    qq = np.arange(128)[None, :]
    tri = np.zeros((128, 2, 128), np.float32)
    tri[:, 0, :] = (kk <= qq)
    tri[:, 1, :] = (kk > qq)
    c["tri"] = ((tri - 1.0) * 240000.0).astype(bf)
    es = np.zeros((128, 16, 128), np.float32)
    for kt in range(16):
        for kk_ in range(128):
            j = 2 * kt + (1 if kk_ >= 64 else 0)
            es[j, kt, kk_] = 1.0
            es[64 + j, kt, kk_] = 1.0
    ef = np.zeros((64, 2048), np.float32)
    for j in range(32):
        ef[j, j * 64:(j + 1) * 64] = 1.0
    c["efull"] = ef.astype(bf)
    p = np.arange(128)[:, None, None]
    t = 128 * qt + p
    cur = t // 64
    j = np.arange(32)[None, None, :]
    forced = (j == 0) | (j == cur) | (j == cur - 1)
    valid = j <= cur
    tka = (valid & ~forced).astype(np.float32)
    tkb = np.where(forced, 8192.0, np.where(valid, 0.0, -8192.0)).astype(np.float32)
    c["tka"] = tka.astype(bf)
    c["tkb"] = tkb.astype(bf)
    c["trilT"] = (kk <= qq).astype(np.float32)
    invcnt = np.zeros((128, 2, 16), np.float32)
    invw = np.zeros((128, 2), np.float32)
    for cc in range(2):
        for pp in range(128):
            w = POOLS[2 * cc + pp // 64]
            invw[pp, cc] = 1.0 / w
            for tt in range(16):
                invcnt[pp, cc, tt] = 1.0 / min(w, tt + 1)
    tp_ = np.arange(128)[:, None]
    tt_ = np.arange(128)[None, :]
    pm = np.zeros((128, 4, 4, 128), np.float32)
    for g_ in range(4):
        w_ = POOLS[g_]
        dd_ = tt_ - tp_
        same = ((dd_ >= 0) & (dd_ < w_)).astype(np.float64)
        pm[:, 0, g_, :] = same / w_ - (dd_ == 0)
        ddp = tt_ + 128 - tp_
        pm[:, 1, g_, :] = ((ddp >= 0) & (ddp < w_)).astype(np.float64) / w_
        first = same / np.minimum(w_, tt_ + 1) - (dd_ == 0)
        hi = first.astype(np.float32).astype(bf).astype(np.float64)
        pm[:, 2, g_, :] = hi
        pm[:, 3, g_, :] = first - hi
    c["pm"] = pm.astype(bf)
    ovl = np.zeros((128, 33), np.float32)
    ovl[:127, 0] = 1.0
    for nn in range(127):
        for jj in range(32):
            if (16 * nn < 64 * jj + 64) and (16 * nn + 32 > 64 * jj):
                ovl[nn, 1 + jj] = 1.0
    c["ovl"] = ovl.astype(bf)
    return c


CONST_SPECS = [
    ("identb", [128, 128], BF16), ("identf", [128, 128], F32), ("mtab", [128, 16, 128], BF16),
    ("mc", [128, 16, 128], BF16), ("tri", [128, 2, 128], BF16),
    ("tka", [128, 16, 32], BF16), ("tkb", [128, 16, 32], BF16), ("trilT", [128, 128], F32),
    ("ovl", [128, 33], BF16),
]
W_SPECS = [
    ("g_pre", [D]), ("w_in", [D, DIN]), ("pe_cmp", [2, 32, 64]), ("w_cmp1", [2, 2048, 256]),
    ("w_cmp2", [2, 256, 64]), ("w_pool", [4, 64, 64]), ("pool_scale", [256]), ("sg_ln_g", [256]),
    ("sg_ln_b", [256]), ("w_sp", [4, 128, 128]), ("b_sp", [4, 128]), ("w_out", [D, D]), ("g_post", [D]),
]


def build(n_layers, phases="CDBA", dbg=False):
    nc = bass.Bass("TRN2", target_bir_lowering=False)
    P = Prog(nc)
    x_d = nc.dram_tensor("x", [S, D], F32, kind="ExternalInput").ap()
    out_d = nc.dram_tensor("out", [S, D], F32, kind="ExternalOutput").ap()
    Wd = {}
    for name, shp in W_SPECS:
        Wd[name] = nc.dram_tensor(name, [n_layers] + shp, F32, kind="ExternalInput").ap()
    Cd = {}
    for name, shp, dt in CONST_SPECS:
        Cd[name] = nc.dram_tensor("c_" + name, shp, dt, kind="ExternalInput").ap()
    Cd["efull"] = nc.dram_tensor("c_efull", [64, 2048], BF16, kind="ExternalInput").ap()
    Cd["pm"] = nc.dram_tensor("c_pm", [128, 4, 4, 128], BF16, kind="ExternalInput").ap()
    if dbg:
        dbg_y = nc.dram_tensor("dbg_y", [128, 8, 2048], BF16, kind="ExternalOutput").ap()

    sb = nc.alloc_sbuf_tensor
    x_sb = sb("x_sb", [128, NT, D], F32)
    hT = sb("hT", [128, KC, S], BF16)
    yT = sb("yT", [128, KC, S], BF16)
    wslot = [sb(f"wslot{i}", [128, 2048], BF16) for i in range(3)]
    C = {}
    for name, shp, dt in CONST_SPECS:
        C[name] = sb("k_" + name, shp, dt)
    gpreT = sb("gpreT", [128, n_layers, 8], F32)
    pscale = sb("pscale", [128, n_layers, 2], F32)
    lngb = sb("lngb", [128, 2, 256], F32)
    Bsp = sb("Bsp", [128, 2, 128], F32)
    wpool_bd = sb("wpool_bd", [128, 2, 128], BF16)
    wpool_f = sb("wpool_f", [128, 2, 128], F32)
    WT = sb("WT", [128, 4, 128], BF16)
    W2k = sb("W2k", [128, 2, 128], BF16)
    W2v = sb("W2v", [128, 2, 64], BF16)
    pe2 = sb("pe2", [128, 2, 16], F32)
    kcT = sb("kcT", [128, 128], BF16)
    vcx = sb("vcx", [128, 97], BF16)
    small = sb("small", [128, 104], F32)
    sm2 = sb("sm2", [128, 176], F32)
    sg = sb("sg", [128, 16, 12], F32)
    NPOOL = 24128
    pool = sb("pool", [128, NPOOL], BF16)
    psall = nc.alloc_psum_tensor("psall", [128, 4096], F32)
    PS = [psall[:, i * 512:(i + 1) * 512] for i in range(8)]

    def carve(off_b, shape, dt, p0=0):
        n = int(np.prod(shape[1:]))
        assert off_b % 4 == 0
        if dt == F32:
            assert off_b // 2 + 2 * n <= NPOOL, (off_b, shape)
            ap = pool[p0:p0 + shape[0], off_b // 2: off_b // 2 + 2 * n].bitcast(F32)
        else:
            assert off_b // 2 + n <= NPOOL, (off_b, shape)
            ap = pool[p0:p0 + shape[0], off_b // 2: off_b // 2 + n]
        if len(shape) == 3:
            ap = ap.rearrange("p (a b) -> p a b", a=shape[1])
        elif len(shape) == 4:
            ap = ap.rearrange("p (a b c) -> p a b c", a=shape[1], b=shape[2])
        return ap

    def psb(bank):
        return PS[bank][:, :].bitcast(BF16)

    for i in range(2):
        P.dma('sp', [(x_sb[:, i, :], x_d[i * 128:(i + 1) * 128, :])], f'x{i}', writes=[('x', i)])
    pairs = [(C[name][:], Cd[name]) for name, _, _ in CONST_SPECS]
    P.dma('sp', pairs, 'const', writes=['const'])
    NG = n_layers * 8
    NP_ = n_layers * 2
    graw = carve(20480, [NG + NP_, 128], F32)
    P.dma('sp', [(graw[0:NG, :], Wd["g_pre"].rearrange("l (kc p) -> (l kc) p", p=128)),
                 (graw[NG:NG + NP_, :], Wd["pool_scale"].rearrange("l (cc p) -> (l cc) p", p=128))],
          'params', writes=['graw'])
    P.op('pe', lambda e: e.transpose(PS[4][:, 0:NG + NP_], graw[:, :], C["identf"][0:NG + NP_, 0:NG + NP_]),
         reads=['graw', 'const'], writes=[('ps', 4)])
    P.op('dve', lambda g: g.tensor_copy(out=gpreT[:, :, :].rearrange("p l k -> p (l k)"), in_=PS[4][:, 0:NG]),
         reads=[('ps', 4)], writes=['params'])
    P.op('dve', lambda g: g.tensor_copy(out=pscale[:, :, :].rearrange("p l k -> p (l k)"), in_=PS[4][:, NG:NG + NP_]),
         reads=[('ps', 4)], writes=['params'])

    for i in range(2, NT):
        P.dma('sp', [(x_sb[:, i, :], x_d[i * 128:(i + 1) * 128, :])], f'x{i}', writes=[('x', i)])
    P.op('dve', lambda g: g.memset(kcT[:], 0.0), writes=['kcT'])
    P.op('dve', lambda g: g.memset(vcx[:], 0.0), writes=['vcx'])
    P.op('dve', lambda g: g.memset(W2k[:], 0.0), writes=['W2k'])
    wstate = {'n': 0, 'issued': 0}
    wjobs = []

    def plan_jobs():
        for l_ in range(n_layers):
            if "C" in phases:
                wjobs.append((l_, 'in', [(C_CZ, 256, 0)]))
                wjobs.append((l_, 'in', [(C_CIN, 256, 0)]))
            if "D" in phases:
                wjobs.append((l_, 'in', [(C_DV, 256, 0)]))
                wjobs.append((l_, 'in', [(C_DZ, 256, 0)]))
                wjobs.append((l_, 'in', [(C_DU, 256, 0)]))
            if "B" in phases:
                wjobs.append((l_, 'in', [(C_BV, 256, 0)]))
                wjobs.append((l_, 'in', [(C_BZ, 256, 0)]))
                wjobs.append((l_, 'in', [(C_BQ, 256, 0)]))
                wjobs.append((l_, 'in', [(C_BK, 256, 0)]))
            if "A" in phases or "a" in phases:
                wjobs.append((l_, 'in', [(C_KC, 64, 0), (C_KC, 64, 64), (C_VC, 64, 128), (C_VC, 64, 192)]))
                for c_ in range(2):
                    for hh_ in range(2):
                        wjobs.append((l_, 'w1', (c_, hh_)))
            if "A" in phases:
                wjobs.append((l_, 'in', [(C_VS, 64, 0), (C_VW, 64, 64), (C_AG, 12, 128)]))
                wjobs.append((l_, 'in', [(C_AZ, 256, 0)]))
                wjobs.append((l_, 'in', [(C_AQ, 256, 0)]))
                wjobs.append((l_, 'in', [(C_AQ + 64, 64, 0), (C_AQ, 64, 64), (C_AQ + 192, 64, 128), (C_AQ + 128, 64, 192)]))
                wjobs.append((l_, 'in', [(C_KS, 64, 0), (C_KS, 64, 64), (C_KW, 64, 128), (C_KW, 64, 192)]))
    plan_jobs()

    def wview(j):
        l_, kind, pay = wjobs[j]
        s_ = j % 3
        if kind == 'in':
            W = sum(sg_[1] for sg_ in pay)
            return s_, wslot[s_][:, 0:8 * W].rearrange("p (kc c) -> p kc c", kc=8)
        return s_, wslot[s_][:, :].rearrange("p (jp h) -> p jp h", jp=16)

    def issue_w(j):
        l_, kind, pay = wjobs[j]
        s_, view = wview(j)
        if kind == 'in':
            prs = [(view[:, :, d0:d0 + ncol], Wd["w_in"][l_][:, c0:c0 + ncol].rearrange("(kc p) c -> p kc c", p=128))
                   for (c0, ncol, d0) in pay]
        else:
            c_, hh_ = pay
            prs = [(view, Wd["w_cmp1"][l_, c_][:, hh_ * 128:(hh_ + 1) * 128].rearrange("(jp p) h -> p jp h", p=128))]
        P.dma('pool', prs, f'w{s_}', writes=[('w', s_)])

    def next_w(l_, kind, pay):
        j = wstate['n']
        wstate['n'] += 1
        assert wjobs[j] == (l_, kind, pay), (j, wjobs[j], (l_, kind, pay))
        while wstate['issued'] < min(len(wjobs), j + 3):
            issue_w(wstate['issued'])
            wstate['issued'] += 1
        return wview(j)

    def load_w(l, segs, src="w_in"):
        return next_w(l, 'in', segs)

    pbank = {'n': 0}

    def proj_fm(s, wv, nch, evac):
        for ch in range(nch):
            for tb in range(4):
                bank = pbank['n'] % 2
                pbank['n'] += 1

                def mm(e, ch=ch, tb=tb, bank=bank):
                    for kc in range(8):
                        last = e.matmul(PS[bank][:, 0:512], lhsT=wv[:, kc, ch * 128:(ch + 1) * 128],
                                        rhs=hT[:, kc, tb * 512:(tb + 1) * 512], start=(kc == 0), stop=(kc == 7))
                    return last
                P.op('pe', mm, reads=[('w', s), 'hT'], writes=[('ps', bank)])
                evac(ch, tb, bank)

    def proj_tm(s, wv, N, evac):
        for i in range(NT):
            bank = pbank['n'] % 2
            pbank['n'] += 1

            def mm(e, i=i, bank=bank):
                for kc in range(8):
                    last = e.matmul(PS[bank][:, 0:N], lhsT=hT[:, kc, i * 128:(i + 1) * 128],
                                    rhs=wv[:, kc, 0:N], start=(kc == 0), stop=(kc == 7))
                return last
            P.op('pe', mm, reads=[('w', s), 'hT'], writes=[('ps', bank)])
            evac(i, bank)

    altc = {'n': 0}

    def alt(engs=('dve', 'act')):
        altc['n'] += 1
        return engs[altc['n'] % len(engs)]

    def copy_op(e, out, in_, reads, writes):
        if e == 'act':
            P.op('act', lambda g: g.activation(out=out, in_=in_, func=AF.Copy), reads=reads, writes=writes)
        else:
            P.op(e, lambda g: g.tensor_copy(out=out, in_=in_), reads=reads, writes=writes)

    for l in range(n_layers):
        hb = [carve(0, [128, 1024], BF16), carve(2048, [128, 1024], BF16)]
        junk = carve(4096, [128, 1024], BF16)
        ss = small[:, 0:16]
        rs = small[:, 16:32]
        rs2 = small[:, 32:48]
        def p0_a(i):
            P.op('act', lambda g, i=i: g.activation(out=junk, in_=x_sb[:, i, :], func=AF.Square,
                                                    accum_out=ss[:, i:i + 1]),
                 reads=[('x', i)], writes=['junk', ('ss', i)])
            P.op('dve', lambda g, i=i: g.tensor_scalar(out=rs[:, i:i + 1], in0=ss[:, i:i + 1], scalar1=1.0 / D,
                                                       scalar2=RMS_EPS, op0=ALU.mult, op1=ALU.add),
                 reads=[('ss', i)], writes=[('rs', i)])
            P.op('act', lambda g, i=i: g.activation(out=rs2[:, i:i + 1], in_=rs[:, i:i + 1], func=AF.Sqrt),
                 reads=[('rs', i)], writes=[('rs2', i)])
            P.op('dve', lambda g, i=i: g.reciprocal(out=rs[:, i:i + 1], in_=rs2[:, i:i + 1]),
                 reads=[('rs2', i)], writes=[('rs', i)])
            b = i % 2
            P.op('act', lambda g, i=i, b=b: g.activation(out=hb[b], in_=x_sb[:, i, :], func=AF.Copy, scale=rs[:, i:i + 1]),
                 reads=[('x', i), ('rs', i)], writes=[('hb', b)])
            bank = 2 + (i % 2)

            def tr(e, b=b, bank=bank):
                for kc in range(8):
                    last = e.transpose(psb(bank)[:, kc * 128:(kc + 1) * 128], hb[b][:, kc * 128:(kc + 1) * 128],
                                       C["identb"][:])
                return last
            P.op('pe', tr, reads=[('hb', b), 'const'], writes=[('ps', bank)])

        def p0_b(i):
            bank = 2 + (i % 2)
            P.op('dve', lambda g, i=i, bank=bank, l=l: g.tensor_tensor(
                out=hT[:, :, i * 128:(i + 1) * 128],
                in0=psb(bank)[:, 0:1024].rearrange("p (a b) -> p a b", a=8),
                in1=gpreT[:, l, :].unsqueeze(2).broadcast_to([128, 8, 128]), op=ALU.mult),
                reads=[('ps', bank), 'params'], writes=['hT', 'wout', 'gpb'])
        p0_a(0)
        for i in range(NT):
            if i + 1 < NT:
                p0_a(i + 1)
            p0_b(i)
        P.barrier()

        if "C" in phases:
            szc = carve(0, [128, 2, 2048], BF16)
            c_tm = carve(8192, [128, 16, 256], BF16)
            pooledT = carve(16384, [128, 2, 2048], BF16)
            pmv = carve(24576, [128, 4, 4, 128], BF16)
            P.dma('sp', [(pmv, Cd["pm"])], f'lp{l}_6', writes=['pmv'])
            P.op('dve', lambda g: g.memset(wpool_f[:], 0.0), writes=['wpool_f'])
            P.dma('sp', [(wpool_f[(g % 2) * 64:(g % 2) * 64 + 64, g // 2, (g % 2) * 64:(g % 2) * 64 + 64],
                          Wd["w_pool"][l, g]) for g in range(4)], f'lp{l}_1', writes=['wpool_f'])
            P.op('dve', lambda g: g.tensor_copy(out=wpool_bd[:], in_=wpool_f[:]), reads=['wpool_f'], writes=['wpool_bd'])
            s, wv = load_w(l, [(C_CZ, 256, 0)])
            proj_fm(s, wv, 2, lambda ch, tb, bank: P.op(
                'act', lambda g: g.activation(out=szc[:, ch, tb * 512:(tb + 1) * 512], in_=PS[bank][:, 0:512], func=AF.Silu),
                reads=[('ps', bank)], writes=['szc']))
            s, wv = load_w(l, [(C_CIN, 256, 0)])
            proj_tm(s, wv, 256, lambda i, bank: copy_op(
                alt(), c_tm[:, i, :], PS[bank][:, 0:256], reads=[('ps', bank)], writes=[('c_tm', i)]))
            for i2 in range(8):
                bank = 2 + i2 % 2
                pp = PS[bank][:, 0:512].rearrange("p (cc t q) -> p cc t q", cc=2, t=2)

                def mmp(e, i2=i2, pp=pp):
                    for ti in range(2):
                        i = 2 * i2 + ti
                        for g_ in range(4):
                            o = pp[(g_ % 2) * 64:(g_ % 2) * 64 + 64, g_ // 2, ti, :]
                            lh = c_tm[:, i, g_ * 64:(g_ + 1) * 64]
                            if i == 0:
                                e.matmul(o, lhsT=lh, rhs=pmv[:, 2, g_, :], start=True, stop=False)
                                last = e.matmul(o, lhsT=lh, rhs=pmv[:, 3, g_, :], start=False, stop=True)
                            else:
                                e.matmul(o, lhsT=lh, rhs=pmv[:, 0, g_, :], start=True, stop=False)
                                last = e.matmul(o, lhsT=c_tm[:, i - 1, g_ * 64:(g_ + 1) * 64], rhs=pmv[:, 1, g_, :],
                                                start=False, stop=True)
                    return last
                P.op('pe', mmp, reads=[('c_tm', j_) for j_ in range(max(0, 2 * i2 - 1), 2 * i2 + 2)] + ['pmv'],
                     writes=[('ps', bank)])
                copy_op(alt(), pooledT[:, :, i2 * 256:(i2 + 1) * 256], PS[bank][:, 0:512].rearrange("p (cc q) -> p cc q", cc=2),
                        reads=[('ps', bank)], writes=[('pooledT', i2 // 2)])
            for cc in range(2):
                for tb in range(4):
                    bank = pbank['n'] % 2
                    pbank['n'] += 1
                    P.op('pe', lambda e, cc=cc, tb=tb, bank=bank: e.matmul(
                        PS[bank][:, 0:512], lhsT=wpool_bd[:, cc, :], rhs=pooledT[:, cc, tb * 512:(tb + 1) * 512],
                        start=True, stop=True), reads=['wpool_bd', ('pooledT', tb)], writes=[('ps', bank)])
                    P.op('dve', lambda g, cc=cc, tb=tb, bank=bank, l=l: g.scalar_tensor_tensor(
                        out=yT[:, 4 + cc, tb * 512:(tb + 1) * 512], in0=PS[bank][:, 0:512],
                        scalar=pscale[:, l, cc:cc + 1], in1=szc[:, cc, tb * 512:(tb + 1) * 512],
                        op0=ALU.mult, op1=ALU.mult),
                        reads=[('ps', bank), 'params', 'szc'], writes=[('yT', 4 + cc)])
            P.barrier()

        if "D" in phases:
            vn = carve(0, [128, 16, 256], BF16)
            ug = carve(8192, [128, 2, 2048], BF16)
            wspf = carve(16384, [128, 4, 128], F32)
            tmpv = [carve(18432 + k * 1024, [128, 256], F32) for k in range(4)]
            tmpz = [carve(22528 + k * 1024, [128, 2, 128], F32) for k in range(4)]
            st6 = small[:, 64:70]
            mv = small[:, 70:72]
            rstd = small[:, 72:73]
            rstd2 = small[:, 73:74]
            P.dma('sp', [(lngb[:, 0, :], Wd["sg_ln_g"][l:l + 1, :].partition_broadcast(128)) if False else
                         (lngb[:, 0, :], Wd["sg_ln_g"][l].partition_broadcast(128)),
                         (lngb[:, 1, :], Wd["sg_ln_b"][l].partition_broadcast(128))]
                  + [(Bsp[(g % 2) * 64:(g % 2) * 64 + 64, g // 2, :], Wd["b_sp"][l, g].partition_broadcast(64))
                     for g in range(4)]
                  + [(wspf[:, g, :], Wd["w_sp"][l, g]) for g in range(4)], f'lp{l}_2', writes=['lngb', 'Bsp', 'wspf'])
            for g_ in range(4):
                bank = 2 + g_ % 2
                P.op('pe', lambda e, g_=g_, bank=bank: e.transpose(PS[bank][:, 0:128], wspf[:, g_, :], C["identf"][:]),
                     reads=['wspf', 'const'], writes=[('ps', bank)])
                P.op('dve', lambda g, g_=g_, bank=bank: g.tensor_tensor(out=WT[:, g_, :], in0=PS[bank][:, 0:128],
                                                                        in1=C["trilT"][:], op=ALU.mult),
                     reads=[('ps', bank), 'const'], writes=['WT'])
            s, wv = load_w(l, [(C_DV, 256, 0)])

            def ev_v(i, bank):
                k = i % 4
                P.op('dve', lambda g: g.bn_stats(out=st6, in_=PS[bank][:, 0:256]), reads=[('ps', bank)], writes=['st6'])
                P.op('dve', lambda g: g.bn_aggr(out=mv, in_=st6), reads=['st6'], writes=['mv'])
                P.op('dve', lambda g: g.tensor_scalar(out=rstd, in0=mv[:, 1:2], scalar1=LN_EPS, scalar2=None, op0=ALU.add),
                     reads=['mv'], writes=['rstd'])
                P.op('act', lambda g: g.activation(out=rstd2, in_=rstd, func=AF.Sqrt), reads=['rstd'], writes=['rstd2'])
                P.op('dve', lambda g: g.reciprocal(out=rstd, in_=rstd2), reads=['rstd2'], writes=['rstd'])
                P.op('dve', lambda g: g.tensor_scalar(out=tmpv[k], in0=PS[bank][:, 0:256], scalar1=mv[:, 0:1],
                                                      scalar2=rstd, op0=ALU.subtract, op1=ALU.mult),
                     reads=[('ps', bank), 'mv', 'rstd'], writes=[('tmpv', k)])
                P.op('pool', lambda g: g.tensor_tensor(out=tmpv[k], in0=tmpv[k], in1=lngb[:, 0, :], op=ALU.mult),
                     reads=[('tmpv', k), 'lngb'], writes=[('tmpv', k)])
                P.op('pool', lambda g: g.tensor_tensor(out=vn[:, i, :], in0=tmpv[k], in1=lngb[:, 1, :], op=ALU.add),
                     reads=[('tmpv', k), 'lngb'], writes=[('vn', i)])
            proj_tm(s, wv, 256, ev_v)
            s2, wv2 = load_w(l, [(C_DZ, 256, 0)])
            proj_fm(s2, wv2, 2, lambda ch, tb, bank: P.op(
                'act', lambda g: g.activation(out=ug[:, ch, tb * 512:(tb + 1) * 512], in_=PS[bank][:, 0:512], func=AF.Silu),
                reads=[('ps', bank)], writes=[('ug', ch, tb)]))
            s3, wv3 = load_w(l, [(C_DU, 256, 0)])
            proj_fm(s3, wv3, 2, lambda ch, tb, bank: P.op(
                'dve', lambda g: g.tensor_tensor(out=ug[:, ch, tb * 512:(tb + 1) * 512], in0=PS[bank][:, 0:512],
                                                 in1=ug[:, ch, tb * 512:(tb + 1) * 512], op=ALU.mult),
                reads=[('ps', bank), ('ug', ch, tb)], writes=[('ug', ch, tb)]))
            for i in range(NT):
                bank = 2 + i % 2
                k = i % 4
                zt = PS[bank][:, 0:256].rearrange("p (a b) -> p a b", a=2)

                def mmz(e, i=i, zt=zt):
                    for g_ in range(4):
                        last = e.matmul(zt[(g_ % 2) * 64:(g_ % 2) * 64 + 64, g_ // 2, :],
                                        lhsT=vn[:, i, g_ * 64:(g_ + 1) * 64], rhs=WT[:, g_, :], start=True, stop=True)
                    return last
                P.op('pe', mmz, reads=[('vn', i), 'WT'], writes=[('ps', bank)])
                P.op('dve', lambda g, zt=zt, k=k: g.tensor_tensor(out=tmpz[k], in0=zt, in1=Bsp[:], op=ALU.add),
                     reads=[('ps', bank), 'Bsp'], writes=[('tmpz', k)])
                tb = i // 4
                P.op('pool', lambda g, i=i, k=k: g.tensor_tensor(out=yT[:, 6:8, i * 128:(i + 1) * 128], in0=tmpz[k],
                                                                 in1=ug[:, :, i * 128:(i + 1) * 128], op=ALU.mult),
                     reads=[('tmpz', k), ('ug', 0, tb), ('ug', 1, tb)], writes=[('yT', 6)])
            P.barrier()

        if "B" in phases:
            QBd = carve(0, [128, 2, 2, 2048], BF16)
            KB = carve(16384, [128, 2, 2048], BF16)
            szB = carve(24576, [128, 2, 2048], BF16)
            VB = carve(32768, [128, 16, 4, 65], BF16)
            PTP = [carve(41088 + k * 2048, [128, 1024], BF16) for k in range(3)]
            obB = [carve(47232 + k * 512, [128, 4, 64], BF16) for k in range(2)]
            rden = small[:, 80:84]
            P.op('dve', lambda g: g.memset(VB[:, :, :, 64:65], 1.0), writes=['VBones'])
            P.op('dve', lambda g: g.memset(QBd[0:64, :, 1, :], 0.0), writes=['QBz'])
            P.op('dve', lambda g: g.memset(QBd[64:128, :, 0, :], 0.0), writes=['QBz'])
            s, wv = load_w(l, [(C_BV, 256, 0)])
            proj_tm(s, wv, 256, lambda i, bank: copy_op(
                alt(), VB[:, i, :, 0:64], PS[bank][:, 0:256].rearrange("p (h d) -> p h d", h=4),
                reads=[('ps', bank)], writes=[('VB', i)]))
            s2, wv2 = load_w(l, [(C_BZ, 256, 0)])
            proj_fm(s2, wv2, 2, lambda ch, tb, bank: P.op(
                'act', lambda g: g.activation(out=szB[:, ch, tb * 512:(tb + 1) * 512], in_=PS[bank][:, 0:512], func=AF.Silu),
                reads=[('ps', bank)], writes=['szB']))
            s3, wv3 = load_w(l, [(C_BQ, 256, 0)])

            def ev_q(ch, tb, bank):
                copy_op('dve', QBd[0:64, ch, 0, tb * 512:(tb + 1) * 512], PS[bank][0:64, 0:512],
                        reads=[('ps', bank), 'QBz'], writes=['QB'])
                copy_op('act', QBd[64:128, ch, 1, tb * 512:(tb + 1) * 512], PS[bank][64:128, 0:512],
                        reads=[('ps', bank), 'QBz'], writes=['QB'])
            proj_fm(s3, wv3, 2, ev_q)
            s4, wv4 = load_w(l, [(C_BK, 256, 0)])
            proj_fm(s4, wv4, 2, lambda ch, tb, bank: copy_op(
                alt(), KB[:, ch, tb * 512:(tb + 1) * 512], PS[bank][:, 0:512], reads=[('ps', bank)], writes=['KB']))

            P.op('pe', lambda e: e.transpose(psb(0)[:, 0:128], C["identb"][:], C["identb"][:]), reads=['const'],
                 writes=[('ps', 0), ('ps0h', 0), ('ps0h', 1)])
            units = []
            for qt in range(NT):
                for kt in range(qt + 1):
                    units.append(dict(qt=qt, kt=kt, first=(kt == 0), last=(kt == qt)))
            LAG = 2
            npairs = (len(units) + 1) // 2
            deferred = []

            def run_deferred_b(step, force=False):
                rest = []
                for (due, fn) in deferred:
                    if force or due <= step:
                        fn()
                    else:
                        rest.append((due, fn))
                deferred[:] = rest
            for step in range(npairs + LAG + 4):
                run_deferred_b(step)
                for us_ in (2 * step, 2 * step + 1):
                    if us_ >= len(units):
                        continue
                    u = units[us_]
                    sbank = 2 + 2 * (step % 3) + (us_ % 2)
                    u['sbank'] = sbank
                    qt, kt = u['qt'], u['kt']

                    def mms(e, sbank=sbank, qt=qt, kt=kt):
                        for c in range(2):
                            last = e.matmul(PS[sbank][:, c * 256:(c + 1) * 256], lhsT=KB[:, c, kt * 128:(kt + 1) * 128],
                                            rhs=QBd[:, c, :, qt * 128:(qt + 1) * 128], start=True, stop=True)
                        return last
                    P.op('pe', mms, reads=['QB', 'QBz', 'KB'], writes=[('ps', sbank)])
                pq = step - LAG
                if pq < 0 or pq >= npairs:
                    continue
                js = [j_ for j_ in (2 * pq, 2 * pq + 1) if j_ < len(units)]
                slot = pq % 3
                nj = len(js)
                P.op('act', lambda g, slot=slot, nj=nj: g.activation(
                    out=PTP[slot][:, 0:nj * 512], in_=psall[:, (2 + 2 * slot) * 512:(2 + 2 * slot + nj) * 512],
                    func=AF.Exp, scale=0.125),
                    reads=[('ps', 2 + 2 * slot + i_) for i_ in range(nj)], writes=[('pt', slot, i_) for i_ in range(nj)])
                for us in js:
                    u = units[us]
                    sbank = u['sbank']
                    pt = PTP[slot][:, (us % 2) * 512:(us % 2 + 1) * 512]
                    ptk = ('pt', slot, us % 2)
                    qt, kt = u['qt'], u['kt']
                    e0 = kt - qt + 15
                    P.op('dve', lambda g, pt=pt, e0=e0: g.tensor_tensor(
                        out=pt[:, 0:512].rearrange("p (j q) -> p j q", j=4),
                        in0=pt[:, 0:512].rearrange("p (j q) -> p j q", j=4),
                        in1=C["mtab"][:, e0, :].unsqueeze(1).broadcast_to([128, 4, 128]), op=ALU.mult),
                        reads=[ptk, 'const'], writes=[ptk])
                    ob = 1
                    ov = PS[ob][:, 0:260].rearrange("p (j d) -> p j d", j=4)

                    def mmo(e, u=u, pt=pt, ov=ov, kt=kt):
                        for j in range(4):
                            last = e.matmul(ov[:, j, :], lhsT=pt[:, j * 128:(j + 1) * 128], rhs=VB[:, kt, j, :],
                                            start=(u['first'] and j == 0), stop=True, skip_group_check=True)
                        return last
                    P.op('pe', mmo, reads=[ptk, ('VB', kt), 'VBones'], writes=[('ps', ob)])
                    if u['last']:
                        k = qt % 2
                        P.op('dve', lambda g, ov=ov: g.reciprocal(out=rden.unsqueeze(2), in_=ov[:, :, 64:65]),
                             reads=[('ps', ob)], writes=['rden'])
                        P.op('dve', lambda g, ov=ov, k=k: g.tensor_tensor(
                            out=obB[k], in0=ov[:, :, 0:64], in1=rden.unsqueeze(2).broadcast_to([128, 4, 64]), op=ALU.mult),
                            reads=[('ps', ob), 'rden'], writes=[('obB', k)])

                        def fin_b(k=k, qt=qt):
                            h0 = (qt % 2) * 512

                            def trb(e):
                                for c in range(2):
                                    last = e.transpose(psb(0)[:, h0 + c * 128:h0 + (c + 1) * 128],
                                                       obB[k][:, 2 * c:2 * c + 2, :].rearrange("p a b -> p (a b)"), C["identb"][:])
                                return last
                            P.op('pe', trb, reads=[('obB', k), 'const'], writes=[('ps0h', qt % 2)])
                            P.op('dve', lambda g: g.tensor_tensor(
                                out=yT[:, 2:4, qt * 128:(qt + 1) * 128],
                                in0=psb(0)[:, h0:h0 + 256].rearrange("p (c q) -> p c q", c=2),
                                in1=szB[:, :, qt * 128:(qt + 1) * 128], op=ALU.mult),
                                reads=[('ps0h', qt % 2), 'szB'], writes=[('yT', 2)])
                        deferred.append((step + 2, fin_b))
            run_deferred_b(0, force=True)
            P.barrier()

        if "A" in phases or "a" in phases:
            kc2 = carve(0, [128, 2, 2048], F32)
            G = carve(16384, [128, 2, 16, 128], BF16)
            hid = carve(24576, [128, 2, 2, 128], BF16)
            w2f = carve(25600, [128, 2, 2, 64], F32)
            with nc.allow_non_contiguous_dma(reason="tiny pe transposes"):
                P.dma('sp', [(pe2[:, c, :], Wd["pe_cmp"][l, c].rearrange("(jp par) d -> (par d) jp", par=2))
                             for c in range(2)]
                      + [(w2f[:, c, :, :], Wd["w_cmp2"][l, c].rearrange("(hh p) d -> p hh d", p=128)) for c in range(2)],
                      f'lp{l}_3', writes=['pe2', 'w2f'])
            P.op('dve', lambda g: g.tensor_copy(out=W2k[:, :, 0:64], in_=w2f[:, 0, :, :]), reads=['w2f'], writes=['W2k'])
            P.op('dve', lambda g: g.tensor_copy(out=W2v[:, :, :], in_=w2f[:, 1, :, :]), reads=['w2f'], writes=['W2v'])
            P.op('dve', lambda g: g.tensor_copy(out=vcx[:, 64:97], in_=C["ovl"][:]), reads=['const'], writes=['vcx'])
            s, wv = load_w(l, [(C_KC, 64, 0), (C_KC, 64, 64), (C_VC, 64, 128), (C_VC, 64, 192)])

            def ev_kc(ch, tb, bank):
                copy_op(alt(), kc2[0:64, ch, tb * 512:(tb + 1) * 512], PS[bank][0:64, 0:512],
                        reads=[('ps', bank)], writes=['kc2'])
                if tb == 0:
                    copy_op(alt(), kc2[64:128, ch, 0:511], PS[bank][64:128, 1:512], reads=[('ps', bank)], writes=['kc2'])
                else:
                    copy_op(alt(), kc2[64:128, ch, tb * 512 - 1:(tb + 1) * 512 - 1], PS[bank][64:128, 0:512],
                            reads=[('ps', bank)], writes=['kc2'])
            proj_fm(s, wv, 2, ev_kc)
            for c in range(2):
                for jp in range(16):
                    P.op('dve', lambda g, c=c, jp=jp: g.tensor_scalar(
                        out=G[:, c, jp, 0:127], in0=kc2[:, c, 2 * jp:2 * jp + 16 * 126 + 1:16],
                        scalar1=pe2[:, c, jp:jp + 1], scalar2=None, op0=ALU.add),
                        reads=['kc2', 'pe2'], writes=[('G', c)])
            for c in range(2):
                for hh in range(2):
                    sl, w1v = next_w(l, 'w1', (c, hh))
                    bank = pbank['n'] % 2
                    pbank['n'] += 1

                    def mmh(e, c=c, w1v=w1v, bank=bank):
                        for jp in range(16):
                            last = e.matmul(PS[bank][:, 0:127], lhsT=w1v[:, jp, :], rhs=G[:, c, jp, 0:127],
                                            start=(jp == 0), stop=(jp == 15))
                        return last
                    P.op('pe', mmh, reads=[('w', sl), ('G', c)], writes=[('ps', bank)])
                    P.op('act', lambda g, c=c, hh=hh, bank=bank: g.activation(
                        out=hid[:, c, hh, 0:127], in_=PS[bank][:, 0:127], func=AF.Gelu_apprx_tanh),
                        reads=[('ps', bank)], writes=[('hid', c)])
            bank = pbank['n'] % 2
            pbank['n'] += 1

            def mmk(e, bank=bank):
                for hh in range(2):
                    last = e.matmul(PS[bank][:, 0:127], lhsT=W2k[:, hh, :], rhs=hid[:, 0, hh, 0:127],
                                    start=(hh == 0), stop=(hh == 1))
                return last
            P.op('pe', mmk, reads=['W2k', ('hid', 0)], writes=[('ps', bank)])
            P.op('dve', lambda g, bank=bank: g.tensor_copy(out=kcT[:, 0:127], in_=PS[bank][:, 0:127]),
                 reads=[('ps', bank)], writes=['kcT'])
            bank = pbank['n'] % 2
            pbank['n'] += 1

            def mmv(e, bank=bank):
                for hh in range(2):
                    last = e.matmul(PS[bank][0:127, 0:64], lhsT=hid[:, 1, hh, 0:127], rhs=W2v[:, hh, :],
                                    start=(hh == 0), stop=(hh == 1))
                return last
            P.op('pe', mmv, reads=['W2v', ('hid', 1)], writes=[('ps', bank)])
            P.op('dve', lambda g, bank=bank: g.tensor_copy(out=vcx[0:127, 0:64], in_=PS[bank][0:127, 0:64]),
                 reads=[('ps', bank)], writes=['vcx'])
            P.barrier()

            if "A" in phases:
                QS = carve(0, [128, 4, 2048], BF16)
                KS = carve(16384, [128, 2048], BF16)
                KW = carve(20480, [128, 2048], BF16)
                szA = carve(24576, [128, 2, 2048], BF16)
                VA = carve(32768, [128, 2, 16, 65], BF16)
                PT = [carve(36928 + k * 1024, [128, 512], BF16) for k in range(2)] + [carve(44608, [128, 512], BF16), carve(46912, [128, 512], BF16)]
                obA = [carve(38976 + k * 512, [128, 4, 64], BF16) for k in range(2)]
                sel2 = [carve(40000 + k * 256, [128, 2, 64], BF16) for k in range(2)]
                ocmp = [carve(40512 + k * 1024, [128, 4, 64], F32) for k in range(2)] + [carve(45632, [128, 4, 64], F32)]
                t1 = carve(42560, [128, 4, 64], F32)
                t2 = carve(43584, [128, 4, 64], F32)
                imp = [sm2[:, 0:32], sm2[:, 32:64]]
                impf = sm2[:, 64:96]
                tmp32 = sm2[:, 96:128]
                m8a = sm2[:, 128:136]
                m8b = sm2[:, 136:144]
                dn = sm2[:, 144:148]
                rdc = sm2[:, 148:152]
                cf1 = sm2[:, 152:156]
                rd2 = sm2[:, 156:160]
                rd3 = sm2[:, 160:164]
                cf2 = sm2[:, 164:168]
                cf3 = sm2[:, 168:172]
                P.op('dve', lambda g: g.memset(VA[:, :, :, 64:65], 1.0), writes=['VAones'])
                for k in range(2):
                    P.op('dve', lambda g, k=k: g.memset(sel2[k][:, :, 32:64], 1.0), writes=[('sel2pad', k)])
                P.dma('sp', [(KS[64:128, :], Cd["efull"])], f'lp{l}_4', writes=['KSe'])
                P.op('dve', lambda g: g.memset(KW[64:128, :], 0.0), writes=['KWz'])
                P.op('dve', lambda g: g.memset(QS[64:128, :, :], 0.0), writes=[('biasT', q_) for q_ in range(NT)])
                s, wv = load_w(l, [(C_VS, 64, 0), (C_VW, 64, 64), (C_AG, 12, 128)])

                def ev_ta(i, bank):
                    copy_op('dve', VA[:, :, i, 0:64], PS[bank][:, 0:128].rearrange("p (a d) -> p a d", a=2),
                            reads=[('ps', bank)], writes=[('VA', i)])
                    P.op('act', lambda g: g.activation(out=sg[:, i, :], in_=PS[bank][:, 128:140], func=AF.Sigmoid),
                         reads=[('ps', bank)], writes=[('sg', i)])
                proj_tm(s, wv, 140, ev_ta)
                s2, wv2 = load_w(l, [(C_AZ, 256, 0)])
                proj_fm(s2, wv2, 2, lambda ch, tb, bank: P.op(
                    'act', lambda g: g.activation(out=szA[:, ch, tb * 512:(tb + 1) * 512], in_=PS[bank][:, 0:512], func=AF.Silu),
                    reads=[('ps', bank)], writes=['szA']))
                s3, wv3 = load_w(l, [(C_AQ, 256, 0)])
                proj_fm(s3, wv3, 2, lambda ch, tb, bank: copy_op(
                    alt(), QS[0:64, 2 * ch, tb * 512:(tb + 1) * 512], PS[bank][0:64, 0:512], reads=[('ps', bank)], writes=['QS']))
                s3b, wv3b = load_w(l, [(C_AQ + 64, 64, 0), (C_AQ, 64, 64), (C_AQ + 192, 64, 128), (C_AQ + 128, 64, 192)])
                proj_fm(s3b, wv3b, 2, lambda ch, tb, bank: copy_op(
                    alt(), QS[0:64, 2 * ch + 1, tb * 512:(tb + 1) * 512], PS[bank][0:64, 0:512], reads=[('ps', bank)], writes=['QS']))
                s4, wv4 = load_w(l, [(C_KS, 64, 0), (C_KS, 64, 64), (C_KW, 64, 128), (C_KW, 64, 192)])
                proj_fm(s4, wv4, 2, lambda ch, tb, bank: copy_op(
                    alt(), (KS if ch == 0 else KW)[0:64, tb * 512:(tb + 1) * 512], PS[bank][0:64, 0:512],
                    reads=[('ps', bank)], writes=['KSW']))

                units = []
                units.append(dict(kind='cmp', qt=0, kt=0))
                units.append(dict(kind='cmp', qt=1, kt=0))
                for qt in range(NT):
                    if qt + 2 < NT:
                        units.append(dict(kind='cmp', qt=qt + 2, kt=0))
                    k0 = max(0, qt - 4)
                    for kt in range(k0, qt + 1):
                        units.append(dict(kind='win', qt=qt, kt=kt, first=(kt == k0), last=(kt == qt)))
                    for kt in range(qt + 1):
                        units.append(dict(kind='slc', qt=qt, kt=kt, first=(kt == 0), last=(kt == qt)))
                import os
                units = units[:int(os.environ.get('A_LIM', '100000'))]
                bias_done = set()
                deferred = []
                ccount = {'slc': 0, 'k64': 0}

                def run_deferred(step, force=False):
                    rest = []
                    for (due, fn) in deferred:
                        if force or due <= step:
                            fn()
                        else:
                            rest.append((due, fn))
                    deferred[:] = rest

                LAG = 3
                for step in range(len(units) + LAG + 12):
                    run_deferred(step)
                    if step < len(units):
                        u = units[step]
                        qt, kt, kind = u['qt'], u['kt'], u['kind']
                        if kind == 'slc' and qt not in bias_done:
                            run_deferred(step, force=True)
                            assert qt in bias_done
                        u['sbank'] = 4 + step % 4
                        u['ptk'] = step % 4
                        sbank = u['sbank']
                        qs = slice(qt * 128, (qt + 1) * 128)
                        if kind == 'cmp':
                            def mmc(e, sbank=sbank, qs=qs, qt=qt):
                                e.matmul(PS[sbank][:, 0:512], lhsT=kcT[:, 0:128], rhs=QS[:, :, qs], start=True, stop=False)
                                return e.matmul(PS[sbank][:, 0:512], lhsT=C["identb"][:],
                                                rhs=C["mc"][:, qt, :].unsqueeze(1).broadcast_to([128, 4, 128]),
                                                start=False, stop=True)
                            P.op('pe', mmc, reads=['QS', 'kcT', ('biasT', qt), 'const'], writes=[('ps', sbank)])
                        elif kind == 'win':
                            mki = 0 if kt == qt else (1 if kt == qt - 4 else None)

                            def mmw(e, sbank=sbank, qs=qs, kt=kt, mki=mki):
                                last = e.matmul(PS[sbank][:, 0:512], lhsT=KW[:, kt * 128:(kt + 1) * 128], rhs=QS[:, :, qs],
                                                start=True, stop=(mki is None))
                                if mki is not None:
                                    last = e.matmul(PS[sbank][:, 0:512], lhsT=C["identb"][:],
                                                    rhs=C["tri"][:, mki, :].unsqueeze(1).broadcast_to([128, 4, 128]),
                                                    start=False, stop=True)
                                return last
                            P.op('pe', mmw, reads=['QS', 'KSW', 'KWz', ('biasT', qt), 'const'], writes=[('ps', sbank)])
                        else:
                            def mmsl(e, sbank=sbank, qs=qs, kt=kt, dg=(kt == qt)):
                                last = e.matmul(PS[sbank][:, 0:512], lhsT=KS[:, kt * 128:(kt + 1) * 128], rhs=QS[:, :, qs],
                                                start=True, stop=(not dg))
                                if dg:
                                    last = e.matmul(PS[sbank][:, 0:512], lhsT=C["identb"][:],
                                                    rhs=C["tri"][:, 0, :].unsqueeze(1).broadcast_to([128, 4, 128]),
                                                    start=False, stop=True)
                                return last
                            P.op('pe', mmsl, reads=['QS', 'KSW', 'KSe', ('biasT', qt), 'const'], writes=[('ps', sbank)])
                    us = step - LAG
                    if us < 0 or us >= len(units):
                        continue
                    u = units[us]
                    qt, kt, kind = u['qt'], u['kt'], u['kind']
                    sbank = u['sbank']
                    pt = PT[u['ptk']]
                    ptk = ('pt', u['ptk'])
                    k = qt % 2
                    P.op('act', lambda g, sbank=sbank, pt=pt: g.activation(out=pt[:, 0:512], in_=PS[sbank][:, 0:512],
                                                                         func=AF.Exp, scale=0.125),
                         reads=[('ps', sbank)], writes=[ptk])
                    if kind == 'cmp':
                        U = PS[1][:, 0:388].rearrange("p (h d) -> p h d", h=4)

                        def mmu(e, pt=pt, U=U):
                            for h in range(4):
                                last = e.matmul(U[:, h, :], lhsT=pt[:, h * 128:(h + 1) * 128], rhs=vcx[:, :],
                                                start=True, stop=True)
                            return last
                        P.op('pe', mmu, reads=[ptk, 'vcx'], writes=['psU'])
                        P.op('dve', lambda g, U=U: g.tensor_scalar(out=dn.unsqueeze(2), in0=U[:, :, 64:65], scalar1=1e-30,
                                                                   scalar2=None, op0=ALU.max), reads=['psU'], writes=['dn'])
                        P.op('dve', lambda g: g.reciprocal(out=rdc, in_=dn), reads=['dn'], writes=['rdc'])
                        P.op('dve', lambda g, qt=qt: g.tensor_tensor(out=cf1, in0=rdc, in1=sg[:, qt, 0:4], op=ALU.mult),
                             reads=['rdc', ('sg', qt)], writes=['cf1'])
                        P.op('dve', lambda g, U=U, k=k: g.tensor_tensor(
                            out=ocmp[qt % 3], in0=U[:, :, 0:64], in1=cf1.unsqueeze(2).broadcast_to([128, 4, 64]), op=ALU.mult),
                            reads=['psU', 'cf1'], writes=[('ocmp', qt % 3)])
                        for h in range(4):
                            if h == 0:
                                P.op('dve', lambda g, U=U, k=k: g.tensor_scalar(out=imp[k], in0=U[:, 0, 65:97], scalar1=rdc[:, 0:1],
                                                                               scalar2=None, op0=ALU.mult),
                                     reads=['psU', 'rdc'], writes=[('imp', k)])
                            else:
                                P.op('dve', lambda g, U=U, k=k, h=h: g.scalar_tensor_tensor(
                                    out=imp[k], in0=U[:, h, 65:97], scalar=rdc[:, h:h + 1], in1=imp[k], op0=ALU.mult, op1=ALU.add),
                                    reads=['psU', 'rdc', ('imp', k)], writes=[('imp', k)])
                        P.op('dve', lambda g, k=k, qt=qt: g.tensor_tensor(out=impf, in0=imp[k], in1=C["tka"][:, qt, :], op=ALU.mult),
                             reads=[('imp', k), 'const'], writes=['impf'])
                        P.op('dve', lambda g, qt=qt: g.tensor_tensor(out=impf, in0=impf, in1=C["tkb"][:, qt, :], op=ALU.add),
                             reads=['impf', 'const'], writes=['impf'])
                        P.op('dve', lambda g: g.max(out=m8a, in_=impf), reads=['impf'], writes=['m8a'])
                        P.op('dve', lambda g: g.match_replace(out=tmp32, in_to_replace=m8a, in_values=impf, imm_value=-1e30),
                             reads=['m8a', 'impf'], writes=['tmp32'])
                        P.op('dve', lambda g: g.max(out=m8b, in_=tmp32), reads=['tmp32'], writes=['m8b'])
                        P.op('dve', lambda g, k=k: g.tensor_scalar(
                            out=sel2[k][:, :, 0:32], in0=impf.unsqueeze(1).broadcast_to([128, 2, 32]),
                            scalar1=m8b[:, 7:8], scalar2=None, op0=ALU.is_ge),
                            reads=['impf', 'm8b'], writes=[('sel2', k)])

                        def fin_bias(k=k, qt=qt):
                            P.op('pe', lambda e: e.transpose(psb(0)[:, 0:128], sel2[k][:, :, :].rearrange("p a b -> p (a b)"),
                                                             C["identb"][:]),
                                 reads=[('sel2', k), ('sel2pad', k), 'const'], writes=['ps0a'])
                            P.op('dve', lambda g: g.tensor_scalar(
                                out=QS[64:128, :, qt * 128:(qt + 1) * 128],
                                in0=psb(0)[64:128, 0:128].unsqueeze(1).broadcast_to([64, 4, 128]),
                                scalar1=-1.0, scalar2=-BIGNEG, op0=ALU.add, op1=ALU.mult),
                                reads=['ps0a'], writes=[('biasT', qt)])
                            bias_done.add(qt)
                        deferred.append((step + 11, fin_bias))
                    else:
                        br = 0 if kind == 'slc' else 1
                        ob = 2 + br
                        ov = PS[ob][:, 0:260].rearrange("p (h d) -> p h d", h=4)

                        def mmo(e, u=u, pt=pt, ov=ov, br=br, kt=kt):
                            for h in range(4):
                                last = e.matmul(ov[:, h, :], lhsT=pt[:, h * 128:(h + 1) * 128], rhs=VA[:, br, kt, :],
                                                start=(u['first'] and h == 0), stop=True, skip_group_check=True)
                            return last
                        P.op('pe', mmo, reads=[ptk, 'VAones', ('VA', kt)], writes=[('ps', ob)])
                        if u['last'] and kind == 'slc':
                            ovs = PS[2][:, 0:260].rearrange("p (h d) -> p h d", h=4)
                            ovw = PS[3][:, 0:260].rearrange("p (h d) -> p h d", h=4)
                            P.op('dve', lambda g, ovs=ovs: g.reciprocal(out=rd2.unsqueeze(2), in_=ovs[:, :, 64:65]),
                                 reads=[('ps', 2)], writes=['rd2'])
                            P.op('dve', lambda g, ovw=ovw: g.reciprocal(out=rd3.unsqueeze(2), in_=ovw[:, :, 64:65]),
                                 reads=[('ps', 3)], writes=['rd3'])
                            P.op('dve', lambda g, qt=qt: g.tensor_tensor(out=cf2, in0=rd2, in1=sg[:, qt, 4:8], op=ALU.mult),
                                 reads=['rd2', ('sg', qt)], writes=['cf2'])
                            P.op('dve', lambda g, qt=qt: g.tensor_tensor(out=cf3, in0=rd3, in1=sg[:, qt, 8:12], op=ALU.mult),
                                 reads=['rd3', ('sg', qt)], writes=['cf3'])
                            P.op('dve', lambda g, ovs=ovs: g.tensor_tensor(
                                out=t1, in0=ovs[:, :, 0:64], in1=cf2.unsqueeze(2).broadcast_to([128, 4, 64]), op=ALU.mult),
                                reads=[('ps', 2), 'cf2'], writes=['t1'])
                            P.op('dve', lambda g, ovw=ovw: g.tensor_tensor(
                                out=t2, in0=ovw[:, :, 0:64], in1=cf3.unsqueeze(2).broadcast_to([128, 4, 64]), op=ALU.mult),
                                reads=[('ps', 3), 'cf3'], writes=['t2'])
                            P.op('dve', lambda g, qt=qt: g.tensor_tensor(out=t1, in0=t1, in1=ocmp[qt % 3], op=ALU.add),
                                 reads=['t1', ('ocmp', qt % 3)], writes=['t1'])
                            P.op('dve', lambda g, k=k: g.tensor_tensor(out=obA[k], in0=t1, in1=t2, op=ALU.add),
                                 reads=['t1', 't2'], writes=[('obA', k)])

                            def fin_out(k=k, qt=qt):
                                def tra(e):
                                    for c in range(2):
                                        last = e.transpose(psb(0)[:, 256 + c * 128:256 + (c + 1) * 128],
                                                           obA[k][:, 2 * c:2 * c + 2, :].rearrange("p a b -> p (a b)"), C["identb"][:])
                                    return last
                                P.op('pe', tra, reads=[('obA', k), 'const'], writes=['ps0b'])
                                P.op('dve', lambda g: g.tensor_tensor(
                                    out=yT[:, 0:2, qt * 128:(qt + 1) * 128],
                                    in0=psb(0)[:, 256:512].rearrange("p (c q) -> p c q", c=2),
                                    in1=szA[:, :, qt * 128:(qt + 1) * 128], op=ALU.mult),
                                    reads=['ps0b', 'szA'], writes=[('yT', 0)])
                            deferred.append((step + 6, fin_out))
                run_deferred(0, force=True)
                P.barrier()

        if dbg and l == 0:
            P.barrier()
            P.dma('sp', [(dbg_y, yT[:])], 'dbg', reads=[])
            P.eng['sp'].wait_ge(P.dsem['dbg'][0], P.dsem['dbg'][1])
            P.barrier()

        wout = hT[:, 0:4, :].rearrange("p a b -> p (a b)").rearrange("p (kc n) -> p kc n", kc=8)
        gpb = hT[:, 4:6, :].rearrange("p a b -> p (a b)").bitcast(F32)[:, 0:1024]
        P.dma('pool', [(wout[:, :, nb * 512:(nb + 1) * 512],
                        Wd["w_out"][l][:, nb * 512:(nb + 1) * 512].rearrange("(kc p) c -> p kc c", p=128))
                       for nb in range(2)], 'wout', writes=['wout', 'hT'])
        P.dma('sp', [(gpb, Wd["g_post"][l].partition_broadcast(128))], f'lp{l}_5', writes=['gpb'])
        tmpo = [carve(k * 4096, [128, 1024], F32) for k in range(2)]
        junk2 = carve(8192, [128, 512], BF16)
        ss2 = small[:, 96:98]
        ss3 = small[:, 98:99]
        ss4 = small[:, 99:100]
        for i in range(NT):
            banks = [(i % 4) * 2, (i % 4) * 2 + 1]
            k = i % 2
            for nb in range(2):
                def mmo2(e, i=i, nb=nb, bank=banks[nb]):
                    for kc in range(8):
                        last = e.matmul(PS[bank][:, 0:512], lhsT=yT[:, kc, i * 128:(i + 1) * 128],
                                        rhs=wout[:, kc, nb * 512:(nb + 1) * 512], start=(kc == 0), stop=(kc == 7))
                    return last
                P.op('pe', mmo2, reads=['wout'] + [('yT', c) for c in range(8)], writes=[('ps', banks[nb])])
                P.op('act', lambda g, nb=nb, bank=banks[nb]: g.activation(out=junk2, in_=PS[bank][:, 0:512], func=AF.Square,
                                                                          accum_out=ss2[:, nb:nb + 1]),
                     reads=[('ps', banks[nb])], writes=['junk2', ('ss2', nb)])
            P.op('dve', lambda g: g.tensor_tensor(out=ss3, in0=ss2[:, 0:1], in1=ss2[:, 1:2], op=ALU.add),
                 reads=[('ss2', 0), ('ss2', 1)], writes=['ss3'])
            P.op('dve', lambda g: g.tensor_scalar(out=ss3, in0=ss3, scalar1=1.0 / D, scalar2=RMS_EPS, op0=ALU.mult, op1=ALU.add),
                 reads=['ss3'], writes=['ss3'])
            P.op('act', lambda g: g.activation(out=ss4, in_=ss3, func=AF.Sqrt), reads=['ss3'], writes=['ss4'])
            P.op('dve', lambda g: g.reciprocal(out=ss3, in_=ss4), reads=['ss4'], writes=['ss3'])
            for nb in range(2):
                P.op('dve', lambda g, nb=nb, k=k, bank=banks[nb]: g.scalar_tensor_tensor(
                    out=tmpo[k][:, nb * 512:(nb + 1) * 512], in0=PS[bank][:, 0:512], scalar=ss3,
                    in1=gpb[:, nb * 512:(nb + 1) * 512], op0=ALU.mult, op1=ALU.mult),
                    reads=[('ps', banks[nb]), 'ss3', 'gpb'], writes=[('tmpo', k)])
            P.op('dve', lambda g, i=i, k=k: g.tensor_tensor(out=x_sb[:, i, :], in0=x_sb[:, i, :], in1=tmpo[k], op=ALU.add),
                 reads=[('tmpo', k), ('x', i)], writes=[('x', i)])
            if l == n_layers - 1:
                P.dma('sp', [(out_d[i * 128:(i + 1) * 128, :], x_sb[:, i, :])], 'out', reads=[('x', i)])
        P.barrier()

    P.eng['sp'].wait_ge(P.dsem['out'][0], P.dsem['out'][1])
    print('PROG counts', P.cnt, {k: v[1] for k, v in P.dsem.items()})
    return nc


_CACHE = {}


def _get_prog(n_layers, phases="CDBA", dbg=False):
    key = (n_layers, phases, dbg)
    if key not in _CACHE:
        _CACHE[key] = build(n_layers, phases, dbg)
    return _CACHE[key]


def kernel(x, g_pre, w_in, pe_cmp, w_cmp1, w_cmp2, w_pool, pool_scale, sg_ln_g, sg_ln_b, w_sp, b_sp, w_out, g_post):
    ws = dict(g_pre=g_pre, w_in=w_in, pe_cmp=pe_cmp, w_cmp1=w_cmp1, w_cmp2=w_cmp2, w_pool=w_pool,
              pool_scale=pool_scale, sg_ln_g=sg_ln_g, sg_ln_b=sg_ln_b, w_sp=w_sp, b_sp=b_sp, w_out=w_out, g_post=g_post)
    ws = {k: np.ascontiguousarray(np.asarray(v, dtype=np.float32)) for k, v in ws.items()}
    x = np.ascontiguousarray(np.asarray(x, dtype=np.float32))
    consts = {"c_" + k: v for k, v in host_consts().items()}
    nc = _get_prog(DEPTH)
    in_maps = []
    for b in range(NCORES):
        m = {"x": x[b]}
        m.update(ws)
        m.update(consts)
        in_maps.append(m)
    res = run_bass_kernel_spmd(nc, in_maps, core_ids=list(range(NCORES)))
    return np.stack([res.results[b]["out"] for b in range(NCORES)], axis=0).astype(np.float32)
```
